# Optimizing a Trainium2 kernel written in Bass

```python
import jax, jax.numpy as jnp
from jax import lax
import numpy as np

D_MODEL = 1024
BATCH = 4
SEQ = 8192
DEPTH = 2

CHUNK = 64
Q_BLOCK = 128
MIX = D_MODEL
W_LRU = MIX // 2
LRU_BLOCKS = 8
LRU_BW = W_LRU // LRU_BLOCKS
LRU_C = 8.0
CONV_A = 4
SB_HEADS = 4
SB_DH = MIX // 4 // SB_HEADS
SB_W = SB_HEADS * SB_DH
DF_HEADS = 4
DF_DV = MIX // 4 // DF_HEADS
DF_DQK = DF_DV // 2
DF_W = DF_HEADS * DF_DV
D_FF = ((8 * D_MODEL // 3 + 255) // 256) * 256
CONV_FF = 3
EPS = 1e-6
IN_SIZES = [W_LRU, W_LRU, SB_W, SB_W, SB_W, DF_W, DF_W, DF_W]
P_IN = sum(IN_SIZES)

kernel_name = 'hybrid_rglru_stickbreak_diffattn_convffn'


def rmsnorm(x, g):
    xf = x.astype(jnp.float32)
    y = xf * lax.rsqrt(jnp.mean(xf * xf, axis=-1, keepdims=True) + EPS)
    return (y * g.astype(jnp.float32)).astype(x.dtype)


def causal_dwconv(x, w, b):
    k = w.shape[0]
    y = lax.conv_general_dilated(
        x, w[:, None, :], window_strides=(1,), padding=[(k - 1, 0)],
        dimension_numbers=('NWC', 'WIO', 'NWC'), feature_group_count=x.shape[-1])
    return y + b


def rg_lru(xa, w_r, b_r, w_i, b_i, lam):
    bn, s, c = xa.shape
    xb = xa.reshape(bn, s, LRU_BLOCKS, LRU_BW)
    r = jax.nn.sigmoid((jnp.einsum('bsnc,ncd->bsnd', xb, w_r).reshape(bn, s, c) + b_r).astype(jnp.float32))
    i = jax.nn.sigmoid((jnp.einsum('bsnc,ncd->bsnd', xb, w_i).reshape(bn, s, c) + b_i).astype(jnp.float32))
    log_a = -LRU_C * r * jax.nn.softplus(-lam.astype(jnp.float32))
    a = jnp.exp(log_a)
    u = jnp.sqrt(-jnp.expm1(2.0 * log_a)) * (i * xa.astype(jnp.float32))

    def combine(left, right):
        a1, b1 = left
        a2, b2 = right
        return a1 * a2, a2 * b1 + b2

    _, h = lax.associative_scan(combine, (a, u), axis=1)
    return h.astype(xa.dtype)


def to_heads(t, n_heads):
    bn, s, w = t.shape
    return t.reshape(bn, s, n_heads, w // n_heads).transpose(0, 2, 1, 3)


def from_blocks(o):
    nb, bn, h, qb, dh = o.shape
    return o.transpose(1, 0, 3, 2, 4).reshape(bn, nb * qb, h, dh)


def stick_breaking(q, k, v):
    s_len = q.shape[2]
    scale = SB_DH ** -0.5
    spos = jnp.arange(s_len)

    def block(start):
        qb = lax.dynamic_slice_in_dim(q, start, Q_BLOCK, axis=2)
        tpos = start + jnp.arange(Q_BLOCK)
        earlier = spos[None, :] < tpos[:, None]
        z = jnp.einsum('bhqd,bhkd->bhqk', qb, k).astype(jnp.float32) * scale
        log_1m = jnp.where(earlier, jax.nn.log_sigmoid(-z), 0.0)
        tail = lax.cumsum(log_1m, axis=3, reverse=True) - log_1m
        wts = jnp.where(earlier, jnp.exp(jax.nn.log_sigmoid(z) + tail), 0.0)
        return jnp.einsum('bhqk,bhkd->bhqd', wts.astype(v.dtype), v)

    starts = jnp.arange(s_len // Q_BLOCK, dtype=jnp.int32) * Q_BLOCK
    o = from_blocks(lax.map(block, starts))
    return o.reshape(o.shape[0], s_len, SB_W)


def diff_attention(q, k, v, lam, lam_init, g_sub):
    s_len = q.shape[2]
    scale = DF_DQK ** -0.5
    q1, q2 = q[..., :DF_DQK], q[..., DF_DQK:]
    k1, k2 = k[..., :DF_DQK], k[..., DF_DQK:]
    slopes = 2.0 ** (-8.0 * (jnp.arange(DF_HEADS, dtype=jnp.float32) + 1.0) / DF_HEADS)
    spos = jnp.arange(s_len)

    def block(start):
        qb1 = lax.dynamic_slice_in_dim(q1, start, Q_BLOCK, axis=2)
        qb2 = lax.dynamic_slice_in_dim(q2, start, Q_BLOCK, axis=2)
        tpos = start + jnp.arange(Q_BLOCK)
        allowed = (spos[None, :] // CHUNK) <= (tpos[:, None] // CHUNK)
        dist = jnp.abs(tpos[:, None] - spos[None, :]).astype(jnp.float32)
        bias = -slopes[:, None, None] * dist

        def probs(qb, kk):
            sc = jnp.einsum('bhqd,bhkd->bhqk', qb, kk).astype(jnp.float32) * scale + bias
            return jax.nn.softmax(jnp.where(allowed, sc, -jnp.inf), axis=-1)

        p = probs(qb1, k1) - lam * probs(qb2, k2)
        return jnp.einsum('bhqk,bhkd->bhqd', p.astype(v.dtype), v)

    starts = jnp.arange(s_len // Q_BLOCK, dtype=jnp.int32) * Q_BLOCK
    o = from_blocks(lax.map(block, starts))
    o = rmsnorm(o, g_sub) * (1.0 - lam_init)
    return o.reshape(o.shape[0], s_len, DF_W)


def setup_inputs(seed: int = 0) -> dict:
    key = jax.random.key(seed)
    ks = jax.random.split(key, 24)
    f32 = jnp.float32

    def nrm(k, shape, fan_in):
        return jax.random.normal(k, shape, f32) * (fan_in ** -0.5)

    def gain(k, shape):
        return 1.0 + 0.02 * jax.random.normal(k, shape, f32)

    def small(k, shape):
        return 0.01 * jax.random.normal(k, shape, f32)

    a_c = jax.random.uniform(ks[8], (DEPTH, W_LRU), f32, minval=0.9, maxval=0.999)
    a_base = a_c ** (1.0 / LRU_C)
    return {
        'x': jax.random.normal(ks[0], (BATCH, SEQ, D_MODEL), f32),
        'norm1_g': gain(ks[1], (DEPTH, D_MODEL)),
        'w_in': nrm(ks[2], (DEPTH, D_MODEL, P_IN), D_MODEL),
        'conv_a_w': nrm(ks[3], (DEPTH, CONV_A, W_LRU), CONV_A),
        'conv_a_b': small(ks[4], (DEPTH, W_LRU)),
        'w_rgate': nrm(ks[5], (DEPTH, LRU_BLOCKS, LRU_BW, LRU_BW), LRU_BW),
        'b_rgate': small(ks[6], (DEPTH, W_LRU)),
        'w_igate': nrm(ks[7], (DEPTH, LRU_BLOCKS, LRU_BW, LRU_BW), LRU_BW),
        'b_igate': small(ks[9], (DEPTH, W_LRU)),
        'lru_lambda': jnp.log(a_base) - jnp.log1p(-a_base),
        'lam_q1': 0.1 * jax.random.normal(ks[10], (DEPTH, DF_DQK), f32),
        'lam_k1': 0.1 * jax.random.normal(ks[11], (DEPTH, DF_DQK), f32),
        'lam_q2': 0.1 * jax.random.normal(ks[12], (DEPTH, DF_DQK), f32),
        'lam_k2': 0.1 * jax.random.normal(ks[13], (DEPTH, DF_DQK), f32),
        'subln_g': gain(ks[14], (DEPTH, DF_DV)),
        'w_out': nrm(ks[15], (DEPTH, MIX, D_MODEL), MIX),
        'norm2_g': gain(ks[16], (DEPTH, D_MODEL)),
        'w_ff_up': nrm(ks[17], (DEPTH, D_MODEL, 2 * D_FF), D_MODEL),
        'conv_ff_w': nrm(ks[18], (DEPTH, CONV_FF, 2 * D_FF), CONV_FF),
        'conv_ff_b': small(ks[19], (DEPTH, 2 * D_FF)),
        'w_ff_down': nrm(ks[20], (DEPTH, D_FF, D_MODEL), D_FF),
        'final_g': gain(ks[21], (D_MODEL,)),
    }


def reference(x, norm1_g, w_in, conv_a_w, conv_a_b, w_rgate, b_rgate, w_igate, b_igate,
              lru_lambda, lam_q1, lam_k1, lam_q2, lam_k2, subln_g, w_out, norm2_g,
              w_ff_up, conv_ff_w, conv_ff_b, w_ff_down, final_g):
    split_at = np.cumsum(IN_SIZES)[:-1].tolist()
    for l in range(DEPTH):
        h = rmsnorm(x, norm1_g[l])
        proj = h @ w_in[l]
        xa, ya, sq, sk, sv, dq, dk, dv = jnp.split(proj, split_at, axis=-1)

        xa = causal_dwconv(xa, conv_a_w[l], conv_a_b[l])
        out_a = rg_lru(xa, w_rgate[l], b_rgate[l], w_igate[l], b_igate[l], lru_lambda[l]) * jax.nn.gelu(ya)

        out_b = stick_breaking(to_heads(sq, SB_HEADS), to_heads(sk, SB_HEADS), to_heads(sv, SB_HEADS))

        lam_init = 0.8 - 0.6 * float(np.exp(-0.3 * l))
        lam = (jnp.exp(jnp.sum(lam_q1[l] * lam_k1[l]).astype(jnp.float32))
               - jnp.exp(jnp.sum(lam_q2[l] * lam_k2[l]).astype(jnp.float32)) + lam_init)
        out_c = diff_attention(to_heads(dq, DF_HEADS), to_heads(dk, DF_HEADS), to_heads(dv, DF_HEADS),
                               lam, lam_init, subln_g[l])

        mixed = jnp.concatenate([out_a, out_b.astype(out_a.dtype), out_c.astype(out_a.dtype)], axis=-1)
        x = x + mixed @ w_out[l]

        h2 = rmsnorm(x, norm2_g[l])
        u = causal_dwconv(h2 @ w_ff_up[l], conv_ff_w[l], conv_ff_b[l])
        gate, val = jnp.split(u, 2, axis=-1)
        x = x + (jax.nn.silu(gate) * val) @ w_ff_down[l]
    return rmsnorm(x, final_g)
```

```python
import math
import numpy as np
import ml_dtypes
from contextlib import ExitStack
import concourse.bass as bass
import concourse.mybir as mybir
from concourse.bass_utils import run_bass_kernel_spmd

F32 = mybir.dt.float32
BF16 = mybir.dt.bfloat16
AF = mybir.ActivationFunctionType
ALU = mybir.AluOpType

D = 1024
W_LRU = 512
P_IN = 2560
D_FF = 2816
EPS = 1e-6
SAME_ENGINE_RAW_SYNC = True


class _Op:
    __slots__ = ("eng", "fn", "deps", "is_dma", "stream", "marked", "ordinal", "sord")

    def __init__(self, eng, fn, is_dma, stream):
        self.eng = eng
        self.fn = fn
        self.deps = []
        self.is_dma = is_dma
        self.stream = stream
        self.marked = False
        self.ordinal = 0
        self.sord = 0


class T:
    def __init__(self, h, name):
        self.h = h
        self.name = name

    def __getitem__(self, idx):
        return self.h[idx]


class Ctx:
    ENGS = ("pe", "act", "dve", "pool", "sp")

    def __init__(self, nc, stack, n_dma_sems=64):
        self.nc = nc
        self.eng_sem = {e: stack.enter_context(nc.semaphore("s_" + e)) for e in self.ENGS}
        self.dma_sems = [stack.enter_context(nc.semaphore("d%d" % i)) for i in range(n_dma_sems)]
        self.first = True

    def clear_block(self):
        nc = self.nc
        with nc.Block() as block:
            @block.sync
            def _(eng):
                for s in list(self.eng_sem.values()) + self.dma_sems:
                    eng.sem_clear(s)


class Sched:
    ENGS = Ctx.ENGS

    def __init__(self, ctx):
        self.ctx = ctx
        self.nc = ctx.nc
        self.ops = {e: [] for e in self.ENGS}
        self.state = {}
        self.stream_cnt = {}
        self.stream_last = {}
        self.nops = 0

    def op(self, eng, fn, reads=(), writes=(), dma=False, stream=None):
        if dma and stream is None:
            stream = writes[0] if len(writes) else reads[0]
        o = _Op(eng, fn, dma, stream)
        deps = {}
        for k in reads:
            st = self.state.get(k)
            if st is not None and st[0] is not None:
                deps[id(st[0])] = (st[0], True)
        for k in writes:
            st = self.state.get(k)
            if st is not None:
                if st[0] is not None and id(st[0]) not in deps:
                    deps[id(st[0])] = (st[0], False)
                for r in st[1]:
                    if id(r) not in deps:
                        deps[id(r)] = (r, False)
        for k in reads:
            st = self.state.get(k)
            if st is None:
                st = [None, []]
                self.state[k] = st
            st[1].append(o)
        for k in writes:
            self.state[k] = [o, []]
        for d, raw in deps.values():
            if d is o:
                continue
            need = True
            if (not d.is_dma) and (not o.is_dma) and d.eng == o.eng:
                need = raw and SAME_ENGINE_RAW_SYNC and o.eng != "pe"
            if need:
                o.deps.append(d)
                d.marked = True
        if dma:
            c = self.stream_cnt.get(stream, 0) + 1
            self.stream_cnt[stream] = c
            o.sord = c
            self.stream_last[stream] = o
        self.ops[eng].append(o)
        self.nops += 1
        return o

    def emit(self):
        ctx = self.ctx
        nc = self.nc
        if not ctx.first:
            ctx.clear_block()
        ctx.first = False
        eng_sem = ctx.eng_sem
        assert len(self.stream_cnt) <= len(ctx.dma_sems), len(self.stream_cnt)
        stream_sem = {k: ctx.dma_sems[i] for i, k in enumerate(self.stream_cnt.keys())}
        for e in self.ENGS:
            n = 0
            for o in self.ops[e]:
                if o.marked and not o.is_dma:
                    n += 1
                    o.ordinal = n
        finals = list(self.stream_last.values())

        def token(d):
            if d.is_dma:
                return stream_sem[d.stream], 16 * d.sord
            return eng_sem[d.eng], d.ordinal

        def run(e, eng):
            waited = {}
            for o in self.ops[e]:
                need = {}
                for d in o.deps:
                    s, v = token(d)
                    key = id(s)
                    if v > waited.get(key, 0) and v > need.get(key, (None, 0))[1]:
                        need[key] = (s, v)
                for key, (s, v) in need.items():
                    eng.wait_ge(s, v)
                    waited[key] = v
                ins = o.fn(eng)
                if o.is_dma:
                    ins.then_inc(stream_sem[o.stream], 16)
                elif o.marked:
                    ins.then_inc(eng_sem[e], 1)
            if e == "sp":
                for d in finals:
                    s, v = token(d)
                    eng.wait_ge(s, v)

        with nc.Block() as block:
            @block.tensor
            def _(eng):
                run("pe", eng)

            @block.scalar
            def _(eng):
                run("act", eng)

            @block.vector
            def _(eng):
                run("dve", eng)

            @block.gpsimd
            def _(eng):
                run("pool", eng)

            @block.sync
            def _(eng):
                run("sp", eng)


def run_pipeline(tasks):
    n = len(tasks)
    if n == 0:
        return
    maxs = max(len(t) for t in tasks)
    for step in range(n + maxs):
        for s in reversed(range(maxs)):
            i = step - s
            if 0 <= i < n and s < len(tasks[i]):
                tasks[i][s]()


def build_program(S, NL, debug_out=None):
    NT = S // 512
    NK = S // 128
    HS = S // 2
    NTD = HS // 512
    NH = 2
    NCA = 2
    PAD = 64
    HALO = 2 * NL
    MB0 = 256
    MC0 = 384
    MOFF = PAD
    PW = 1280
    C_XA, C_YA, C_SQ, C_SK, C_SV, C_DQ, C_DK, C_DV = 0, 256, 512, 640, 768, 896, 1024, 1152
    GROUPS = [[0, 1], [2, 3], [4, 5], [6, 7]]
    nc = bass.Bass("TRN2", target_bir_lowering=False)

    def din(name, shape, dt=F32):
        return nc.dram_tensor(name, list(shape), dt, kind="ExternalInput").ap()

    def dscr(name, shape, dt):
        return nc.dram_tensor(name, list(shape), dt).ap()

    xT = din("xT", [D, S])
    xh = din("xh", [D, PAD + HS])
    norm1_g = din("norm1_g", [NL, 128, 8]); w_in = din("w_in", [NL, D, PW])
    conv_a_w = din("conv_a_w", [NL, 128, 4, NCA]); conv_a_b = din("conv_a_b", [NL, 128, NCA])
    w_rgate = din("w_rgate", [NL, 4, 64, 64]); b_rgate = din("b_rgate", [NL, 128, NCA])
    w_igate = din("w_igate", [NL, 4, 64, 64]); b_igate = din("b_igate", [NL, 128, NCA])
    lru_lambda = din("lru_lambda", [NL, 128, NCA])
    lam_q1 = din("lam_q1", [NL, 32]); lam_k1 = din("lam_k1", [NL, 32])
    lam_q2 = din("lam_q2", [NL, 32]); lam_k2 = din("lam_k2", [NL, 32])
    subln_g = din("subln_g", [NL, 64]); w_out = din("w_out", [NL, D, D])
    norm2_g = din("norm2_g", [NL, 128, 8]); w_ff_up = din("w_ff_up", [NL, D, 2 * D_FF])
    conv_ff_w = din("conv_ff_w", [NL, 128, 3, 44]); conv_ff_b = din("conv_ff_b", [NL, 128, 44])
    w_ff_down = din("w_ff_down", [NL, D_FF, D]); final_g = din("final_g", [128, 8])
    c_tri = din("c_tri", [128, 128], BF16)
    c_sbmask = din("c_sbmask", [128, 128])
    c_dfmask = din("c_dfmask", [NH, 128, 128])
    c_qaug = din("c_qaug", [NH, 4, S], BF16)
    c_kaug = din("c_kaug", [NH, 4, S], BF16)
    jmask = din("jmask", [128, 1])

    outT = nc.dram_tensor("outT", [D, HS], F32, kind="ExternalOutput").ap()
    dbg = None
    if debug_out is not None:
        dbg = nc.dram_tensor("dbg", list(debug_out[1]), debug_out[2], kind="ExternalOutput").ap()

    xbuf = dscr("xbuf", [D, PAD + HS], F32)
    hmain_t = [nc.dram_tensor("hmain%d" % i, [256, HS], BF16) for i in range(4)]
    hg_t = [nc.dram_tensor("hg%d" % i, [512, HS], BF16) for i in range(4)]
    xmidT = dscr("xmidT", [D, PAD + HS], F32)
    mixo_t = [nc.dram_tensor("mixo%d" % i, [64, PAD + S], BF16) for i in range(8)]
    mixg_t = [nc.dram_tensor("mixg%d" % i, [128, PAD + S], BF16) for i in range(8)]
    sqT = dscr("sqT", [128, S], BF16)
    skT = dscr("skT", [128, S], BF16)
    dqT = dscr("dqT", [NH, 2, 32, S], BF16)
    dkT = dscr("dkT", [NH, 2, 32, S], BF16)
    actT = dscr("actT", [D_FF, PAD + HS], BF16)

    top = ExitStack()
    ctx = Ctx(nc, top)
    cc_sem = top.enter_context(nc.semaphore("cc_sem"))
    cc_count = [0]
    _cnt = [0]

    def all_gather(src_ts, dst_ts):
        with nc.Block() as block:
            @block.gpsimd
            def _(g):
                for src_t, dst_t in zip(src_ts, dst_ts):
                    g.collective_compute("AllGather", ALU.bypass, replica_groups=GROUPS,
                                         ins=[src_t.ap().opt()], outs=[dst_t.ap().opt()]).then_inc(cc_sem)
                    cc_count[0] += 1
                g.wait_ge(cc_sem, cc_count[0])

    def uniq(name):
        _cnt[0] += 1
        return "%s_%d" % (name, _cnt[0])

    def chunked(ap2d):
        return ap2d.rearrange("(c p) t -> p c t", p=128)

    def vec_pc(ap1d):
        return ap1d.rearrange("(c p) -> p c", p=128)

    for l in range(NL):
        lam_init = 0.8 - 0.6 * float(np.exp(-0.3 * l))
        last = (l == NL - 1)

        with ExitStack() as st:
            def sb(name, shape, dt):
                return T(st.enter_context(nc.sbuf_tensor(uniq(name), list(shape), dt)), name)

            def ps(name, shape=(128, 512), dt=F32):
                return T(st.enter_context(nc.psum_tensor(uniq(name), list(shape), dt)), name)

            Vs = sb("Vs", [128, NK, NH * 64], BF16)
            Vd = sb("Vd", [128, NK, NH * 65 + 64], BF16)
            ones_bf = sb("ones_bf", [128, 128], BF16)
            tri = sb("tri", [128, 128], BF16)

            with ExitStack() as st2:
                def sb2(name, shape, dt):
                    return T(st2.enter_context(nc.sbuf_tensor(uniq(name), list(shape), dt)), name)

                def ps2(name, shape=(128, 512), dt=F32):
                    return T(st2.enter_context(nc.psum_tensor(uniq(name), list(shape), dt)), name)

                S_ = Sched(ctx)
                win = sb2("win", [128, 8, PW], BF16)
                wstage = [sb2("wstage%d" % i, [128, PW // 2], F32) for i in range(2)]
                g1 = sb2("g1", [128, 8], F32)
                caw = sb2("caw", [128, 4, NCA], F32)
                cab = sb2("cab", [128, NCA], F32)
                brg = sb2("brg", [128, NCA], F32)
                big = sb2("big", [128, NCA], F32)
                lam_t = sb2("lam_t", [128, NCA], F32)
                nsp = sb2("nsp", [128, NCA], F32)
                wr_st = sb2("wr_st", [128, NCA, 128], F32)
                wi_st = sb2("wi_st", [128, NCA, 128], F32)
                wr_bf = sb2("wr_bf", [128, NCA, 128], BF16)
                wi_bf = sb2("wi_bf", [128, NCA, 128], BF16)
                xt = [sb2("xt%d" % i, [128, 8, 512], F32) for i in range(1)]
                sqb = sb2("sqb", [128, 8, 512], BF16)
                rstd = sb2("rstd", [128, 512], F32)
                hT = sb2("hT", [128, 8, 512], BF16)
                xa_ext = [sb2("xa_ext%d" % c, [128, 3 + 512], F32) for c in range(NCA)]
                carry_a = [sb2("carry_a%d" % c, [128, 3], F32) for c in range(NCA)]
                state = [sb2("state%d" % c, [128, 1], F32) for c in range(NCA)]
                zpad = sb2("zpad", [128, PAD], BF16)
                qk_st = [sb2("qk_st%d" % i, [128, 512], BF16) for i in range(4)]
                pp = [ps2("pp%d" % i) for i in range(4)]
                pss = ps2("pss")
                pg = [ps2("pg%d" % i) for i in range(2)]

                if l > 0:
                    def gather_h(g):
                        ins = None
                        for src_t, dst_t in zip(hmain_t, hg_t):
                            ins = g.collective_compute("AllGather", ALU.bypass, replica_groups=GROUPS,
                                                       ins=[src_t.ap().opt()], outs=[dst_t.ap().opt()]).then_inc(cc_sem)
                            cc_count[0] += 1
                        return ins
                    S_.op("pool", gather_h)
                S_.op("pool", lambda e: e.memset(ones_bf[:], 1.0), writes=[ones_bf])
                S_.op("pool", lambda e: e.memset(zpad[:], 0.0), writes=[zpad])
                for r in range(8):
                    S_.op("sp", (lambda r=r: lambda e: e.dma_start(out=mixo_t[r].ap()[:, 0:PAD], in_=zpad[0:64, :]))(), reads=[zpad], writes=[("mixpad", r)], dma=True, stream=zpad)
                S_.op("sp", lambda e: e.dma_start(out=tri[:], in_=c_tri), writes=[tri], dma=True)
                S_.op("sp", lambda e: e.dma_start(out=g1[:], in_=norm1_g[l]), writes=[g1], dma=True)
                S_.op("sp", lambda e: e.dma_start(out=caw[:], in_=conv_a_w[l]), writes=[caw], dma=True)
                S_.op("sp", lambda e: e.dma_start(out=cab[:], in_=conv_a_b[l]), writes=[cab], dma=True)
                S_.op("sp", lambda e: e.dma_start(out=brg[:], in_=b_rgate[l]), writes=[brg], dma=True)
                S_.op("sp", lambda e: e.dma_start(out=big[:], in_=b_igate[l]), writes=[big], dma=True)
                S_.op("sp", lambda e: e.dma_start(out=lam_t[:], in_=lru_lambda[l]), writes=[lam_t], dma=True)
                S_.op("act", lambda e: e.activation(out=nsp[:], in_=lam_t[:], func=AF.Exp, scale=-1.0), reads=[lam_t], writes=[nsp])
                S_.op("act", lambda e: e.activation(out=nsp[:], in_=nsp[:], func=AF.Ln, bias=1.0, scale=1.0), reads=[nsp], writes=[nsp])
                S_.op("dve", lambda e: e.tensor_scalar(out=nsp[:], in0=nsp[:], scalar1=-8.0, scalar2=None, op0=ALU.mult), reads=[nsp], writes=[nsp])
                S_.op("pool", lambda e: e.memset(wr_st[:], 0.0), writes=[wr_st])
                S_.op("pool", lambda e: e.memset(wi_st[:], 0.0), writes=[wi_st])
                for c in range(NCA):
                    for hb in range(2):
                        n = 2 * c + hb
                        S_.op("sp", (lambda c=c, hb=hb, n=n: lambda e: e.dma_start(out=wr_st[hb * 64:(hb + 1) * 64, c, hb * 64:(hb + 1) * 64], in_=w_rgate[l, n]))(), reads=[], writes=[wr_st], dma=True, stream=("wr", 0))
                        S_.op("sp", (lambda c=c, hb=hb, n=n: lambda e: e.dma_start(out=wi_st[hb * 64:(hb + 1) * 64, c, hb * 64:(hb + 1) * 64], in_=w_igate[l, n]))(), reads=[], writes=[wi_st], dma=True, stream=("wi", 0))
                S_.op("dve", lambda e: e.tensor_copy(out=wr_bf[:], in_=wr_st[:]), reads=[wr_st], writes=[wr_bf])
                S_.op("dve", lambda e: e.tensor_copy(out=wi_bf[:], in_=wi_st[:]), reads=[wi_st], writes=[wi_bf])
                for kc in range(8):
                    for hf in range(2):
                        stg = wstage[hf]
                        S_.op("sp", (lambda kc=kc, hf=hf, stg=stg: lambda e: e.dma_start(out=stg[:], in_=w_in[l, kc * 128:(kc + 1) * 128, hf * 640:(hf + 1) * 640]))(), writes=[stg], dma=True)
                        S_.op("act" if hf else "dve", (lambda kc=kc, hf=hf, stg=stg: lambda e: (e.copy if hf else e.tensor_copy)(out=win[:, kc, hf * 640:(hf + 1) * 640], in_=stg[:]))(), reads=[stg], writes=[(win, kc)])
                win_keys = [(win, kc) for kc in range(8)]
                for c in range(NCA):
                    S_.op("pool", (lambda c=c: lambda e: e.memset(carry_a[c][:], 0.0))(), writes=[carry_a[c]])
                    S_.op("pool", (lambda c=c: lambda e: e.memset(state[c][:], 0.0))(), writes=[state[c]])
                S_.op("pool", lambda e: e.memset(Vd[:], 1.0), writes=[("Vd", "all")])

                NSET = 2
                lset = [dict(ya=sb2("ya_%d" % i, [128, 512], F32), xc=sb2("xc_%d" % i, [128, 512], F32),
                             rg=sb2("rg_%d" % i, [128, 512], F32), ig=sb2("ig_%d" % i, [128, 512], F32),
                             av=sb2("av_%d" % i, [128, 512], F32), t1=sb2("t1_%d" % i, [128, 512], F32),
                             t2=sb2("t2_%d" % i, [128, 512], F32), xcb=sb2("xcb_%d" % i, [128, 512], BF16),
                             oa=sb2("oa_%d" % i, [128, 512], BF16)) for i in range(NSET)]
                xt2 = [xt[0], sb2("xt_b", [128, 8, 512], F32)]
                hT2 = [hT, sb2("hT_b", [128, 8, 512], BF16)]
                sqb2 = [sqb, sqb]
                rstd2 = [rstd, sb2("rstd_b", [128, 512], F32)]
                pss2 = [pss, ps2("pss_b")]
                npp = [0]
                nls = [0]

                def norm_task(ti):
                    t0 = ti * 512
                    xtt, hTt, sqt, rst, pst = xt2[ti % 2], hT2[ti % 2], sqb2[ti % 2], rstd2[ti % 2], pss2[ti % 2]

                    if l > 0:
                        def n0h():
                            hf_ = ti // NTD
                            tc0 = (ti % NTD) * 512
                            def ld_h(c, r0):
                                def fn(e):
                                    if ti == 0 and c == 0:
                                        e.wait_ge(cc_sem, cc_count[0])
                                    return e.dma_start(out=hTt[:, c, :], in_=hg_t[c // 2].ap()[r0:r0 + 128, tc0:tc0 + 512])
                                return fn
                            for c in range(8):
                                r0 = hf_ * 256 + (c % 2) * 128
                                S_.op("sp", ld_h(c, r0), writes=[(hTt, c)], dma=True, stream=(hTt, c))
                        return [n0h]

                    def n0():
                        xsrc = chunked(xT)[:, :, t0:t0 + 512]
                        S_.op("sp", lambda e: e.dma_start(out=xtt[:], in_=xsrc), writes=[xtt], dma=True)

                    def n1():
                        for c in range(8):
                            S_.op("pool", (lambda c=c: lambda e: e.tensor_tensor(out=sqt[:, c, :], in0=xtt[:, c, :], in1=xtt[:, c, :], op=ALU.mult))(), reads=[xtt] + [(xtt, "c", cc) for cc in range(8)], writes=[(sqt, c)])

                    def n2():
                        for c in range(8):
                            S_.op("pe", (lambda c=c: lambda e: e.matmul(pst[:], lhsT=ones_bf[:], rhs=sqt[:, c, :], start=(c == 0), stop=(c == 7)))(), reads=[ones_bf, (sqt, c)], writes=[pst])

                    def n3():
                        S_.op("act", lambda e: e.activation(out=rst[:], in_=pst[:], func=AF.Sqrt, bias=EPS, scale=1.0 / D), reads=[pst], writes=[rst])

                    def n4():
                        S_.op("dve", lambda e: e.reciprocal(out=rst[:], in_=rst[:]), reads=[rst], writes=[rst])

                    def n5():
                        for c in range(8):
                            S_.op("dve", (lambda c=c: lambda e: e.scalar_tensor_tensor(out=hTt[:, c, :], in0=xtt[:, c, :], scalar=g1[:, c:c + 1], in1=rst[:], op0=ALU.mult, op1=ALU.mult))(), reads=[xtt, g1, rst] + [(xtt, "c", cc) for cc in range(8)], writes=[(hTt, c)])
                    return [n0, n1, n2, n3, n4, n5]

                def proj_ops(pt, col0, hTt):
                    for kc in range(8):
                        S_.op("pe", (lambda kc=kc: lambda e: e.matmul(pt[:], lhsT=win[:, kc, col0:col0 + 128], rhs=hTt[:, kc, :], start=(kc == 0), stop=(kc == 7)))(), reads=[(win, kc), (hTt, kc)], writes=[pt])

                def qk_task(ti, nm, cbase, scl):
                    t0 = ti * 512
                    hTt = hT2[ti % 2]
                    pt = pp[npp[0] % 4]
                    stt = qk_st[npp[0] % 4]
                    npp[0] += 1

                    def q0():
                        proj_ops(pt, cbase, hTt)

                    def q1():
                        S_.op("act", lambda e: e.activation(out=stt[:], in_=pt[:], func=AF.Identity, scale=scl), reads=[pt], writes=[stt])

                    def q2():
                        if nm == "sq" or nm == "sk":
                            dst = sqT if nm == "sq" else skT
                            S_.op("sp", lambda e: e.dma_start(out=dst[0:128, t0:t0 + 512], in_=stt[:]), reads=[stt], writes=[(nm, ti)], dma=True, stream=stt)
                        else:
                            dst = dqT if nm == "dq" else dkT
                            for hh_ in range(2):
                                for m in range(2):
                                    r0 = hh_ * 64 + m * 32
                                    S_.op("sp", (lambda hh_=hh_, m=m, r0=r0: lambda e: e.dma_start(out=dst[hh_, m, :, t0:t0 + 512], in_=stt[r0:r0 + 32, :]))(), reads=[stt], writes=[(nm, ti, hh_, m)], dma=True, stream=stt)
                    return [q0, q1, q2]

                def v_task(ti, sub):
                    hTt = hT2[ti % 2]
                    pt = pp[npp[0] % 4]
                    npp[0] += 1
                    tt = ti * 4 + sub

                    def v0():
                        for half, cbase in enumerate((C_SV, C_DV)):
                            for kc in range(8):
                                S_.op("pe", (lambda kc=kc, half=half, cbase=cbase: lambda e: e.matmul(pt[:, half * 128:(half + 1) * 128], lhsT=hTt[:, kc, sub * 128:(sub + 1) * 128], rhs=win[:, kc, cbase:cbase + 128], start=(kc == 0), stop=(kc == 7)))(), reads=[(win, kc), (hTt, kc)], writes=[pt])

                    def v1():
                        S_.op("act", lambda e: e.copy(out=Vs[:, tt, :], in_=pt[:, 0:128]), reads=[pt], writes=[("Vs", ti, sub)])
                        S_.op("act", lambda e: e.copy(out=Vd[:, tt, 0:NH * 65].rearrange("p (h d) -> p h d", d=65)[:, :, 0:64], in_=pt[:, 128:256].rearrange("p (h d) -> p h d", d=64)), reads=[pt, ("Vd", "all")], writes=[("Vd", ti, sub)])
                    return [v0, v1]

                def lru_task(ti, c):
                    t0 = ti * 512
                    hTt = hT2[ti % 2]
                    pxa = pp[npp[0] % 4]
                    pya = pp[(npp[0] + 1) % 4]
                    npp[0] += 2
                    B_ = lset[nls[0] % NSET]
                    nls[0] += 1
                    ya, xc, rg, ig, av, t1, t2, xcb, oat = B_["ya"], B_["xc"], B_["rg"], B_["ig"], B_["av"], B_["t1"], B_["t2"], B_["xcb"], B_["oa"]
                    ext = xa_ext[c]

                    def a0():
                        proj_ops(pxa, C_XA + c * 128, hTt)
                        proj_ops(pya, C_YA + c * 128, hTt)

                    def a1():
                        S_.op("act", lambda e: e.copy(out=ext[:, 3:515], in_=pxa[:]), reads=[pxa], writes=[(ext, "m")])
                        S_.op("act", lambda e: e.copy(out=ya[:], in_=pya[:]), reads=[pya], writes=[ya])
                        S_.op("pool", lambda e: e.tensor_copy(out=ext[:, 0:3], in_=carry_a[c][:]), reads=[carry_a[c]], writes=[(ext, "h")])

                    def a2():
                        S_.op("dve", lambda e: e.tensor_scalar(out=xc[:], in0=ext[:, 3:515], scalar1=caw[:, 3, c:c + 1], scalar2=cab[:, c:c + 1], op0=ALU.mult, op1=ALU.add), reads=[(ext, "m"), caw, cab], writes=[xc])
                        S_.op("pool", lambda e: e.tensor_tensor(out=t1[:], in0=ya[:], in1=ya[:], op=ALU.mult), reads=[ya], writes=[t1])

                    def a3():
                        for k in range(3):
                            S_.op("dve", (lambda k=k: lambda e: e.scalar_tensor_tensor(out=xc[:], in0=ext[:, k:k + 512], scalar=caw[:, k, c:c + 1], in1=xc[:], op0=ALU.mult, op1=ALU.add))(), reads=[(ext, "m"), (ext, "h"), caw, xc], writes=[xc])
                        S_.op("pool", lambda e: e.tensor_scalar(out=t1[:], in0=t1[:], scalar1=0.044715, scalar2=1.0, op0=ALU.mult, op1=ALU.add), reads=[t1], writes=[t1])

                    def a4():
                        S_.op("pool", lambda e: e.tensor_copy(out=carry_a[c][:], in_=ext[:, 512:515]), reads=[(ext, "m")], writes=[carry_a[c]])
                        S_.op("pool", lambda e: e.tensor_copy(out=xcb[:], in_=xc[:]), reads=[xc], writes=[xcb])
                        S_.op("pool", lambda e: e.tensor_tensor(out=t1[:], in0=t1[:], in1=ya[:], op=ALU.mult), reads=[t1, ya], writes=[t1])

                    def a5():
                        S_.op("pe", lambda e: e.matmul(pg[0][:], lhsT=wr_bf[:, c, :], rhs=xcb[:], start=True, stop=True), reads=[wr_bf, xcb], writes=[pg[0]])
                        S_.op("pe", lambda e: e.matmul(pg[1][:], lhsT=wi_bf[:, c, :], rhs=xcb[:], start=True, stop=True), reads=[wi_bf, xcb], writes=[pg[1]])

                    def a6():
                        S_.op("act", lambda e: e.activation(out=rg[:], in_=pg[0][:], func=AF.Sigmoid, bias=brg[:, c:c + 1], scale=1.0), reads=[pg[0], brg], writes=[rg])
                        S_.op("act", lambda e: e.activation(out=ig[:], in_=pg[1][:], func=AF.Sigmoid, bias=big[:, c:c + 1], scale=1.0), reads=[pg[1], big], writes=[ig])
                        S_.op("act", lambda e: e.activation(out=t2[:], in_=t1[:], func=AF.Sigmoid, scale=1.5957691216057308), reads=[t1], writes=[t2])

                    def a7():
                        S_.op("act", lambda e: e.activation(out=av[:], in_=rg[:], func=AF.Exp, scale=nsp[:, c:c + 1]), reads=[rg, nsp], writes=[av])
                        S_.op("dve", lambda e: e.tensor_tensor(out=ig[:], in0=ig[:], in1=xc[:], op=ALU.mult), reads=[ig, xc], writes=[ig])
                        S_.op("pool", lambda e: e.tensor_tensor(out=t2[:], in0=t2[:], in1=ya[:], op=ALU.mult), reads=[t2, ya], writes=[t2])

                    def a8():
                        S_.op("pool", lambda e: e.tensor_tensor(out=rg[:], in0=av[:], in1=av[:], op=ALU.mult), reads=[av], writes=[rg])

                    def a9():
                        S_.op("act", lambda e: e.activation(out=rg[:], in_=rg[:], func=AF.Sqrt, bias=1.0, scale=-1.0), reads=[rg], writes=[rg])

                    def a10():
                        S_.op("dve", lambda e: e.tensor_tensor(out=ig[:], in0=ig[:], in1=rg[:], op=ALU.mult), reads=[ig, rg], writes=[ig])

                    def a11():
                        S_.op("dve", lambda e: e.tensor_tensor_scan(out=xc[:], data0=av[:], data1=ig[:], initial=state[c][:, 0:1], op0=ALU.mult, op1=ALU.add), reads=[av, ig, state[c]], writes=[xc])

                    def a12():
                        S_.op("pool", lambda e: e.tensor_copy(out=state[c][:], in_=xc[:, 511:512]), reads=[xc], writes=[state[c]])
                        S_.op("dve", lambda e: e.tensor_tensor(out=oat[:], in0=t2[:], in1=xc[:], op=ALU.mult), reads=[t2, xc], writes=[oat])

                    def a13():
                        for hq in range(2):
                            S_.op("sp", (lambda hq=hq: lambda e: e.dma_start(out=mixo_t[2 * c + hq].ap()[:, MOFF + t0:MOFF + t0 + 512], in_=oat[hq * 64:(hq + 1) * 64, :]))(), reads=[oat], writes=[("mixed_a", ti, c, hq)], dma=True, stream=oat)
                    return [a0, a1, a2, a3, a4, a5, a6, a7, a8, a9, a10, a11, a12, a13]

                qk_specs = [("sq", C_SQ, 0.125), ("sk", C_SK, 1.0), ("dq", C_DQ, 32 ** -0.5), ("dk", C_DK, 1.0)]
                tasks = [norm_task(0)]
                for ti in range(NT):
                    if ti + 1 < NT:
                        tasks.append(norm_task(ti + 1))
                    if ti == 0:
                        tasks += [[] for _ in range(6)]
                    for c in range(NCA):
                        tasks.append(lru_task(ti, c))
                    for (nm, cbase, scl) in qk_specs:
                        tasks.append(qk_task(ti, nm, cbase, scl))
                    for sub in range(4):
                        tasks.append(v_task(ti, sub))
                run_pipeline(tasks)
                S_.emit()

            with ExitStack() as st2:
                def sb2(name, shape, dt):
                    return T(st2.enter_context(nc.sbuf_tensor(uniq(name), list(shape), dt)), name)

                def ps2(name, shape=(128, 512), dt=F32):
                    return T(st2.enter_context(nc.psum_tensor(uniq(name), list(shape), dt)), name)

                S_ = Sched(ctx)
                kT = [sb2("kT%d" % i, [64, S], BF16) for i in range(2)]
                qT = [sb2("qT%d" % i, [64, S], BF16) for i in range(2)]
                sbm = sb2("sbm", [128, 128], F32)
                zeros_bf = sb2("zeros_bf", [128, 64], BF16)
                NE = 7
                E = [sb2("E%d" % i, [128, 2, 512], F32) for i in range(NE)]
                L = [sb2("L%d" % i, [128, 2, 512], BF16) for i in range(3)]
                tmp = [sb2("tmp%d" % i, [128, 2, 512], F32) for i in range(3)]
                eC = [sb2("eC%d" % i, [128, 2, 512], F32) for i in range(3)]
                Wt = [sb2("Wt%d" % i, [128, 2, 512], BF16) for i in range(3)]
                carry = sb2("carry", [128, 512], F32)
                osb = [sb2("osb%d" % i, [64, 512], BF16) for i in range(2)]
                pz = [ps2("pz%d" % i, (128, 2, 512)) for i in range(2)]
                pc = ps2("pc", (128, 2, 512))
                pb = ps2("pb")
                pot = ps2("po")
                S_.op("sp", lambda e: e.dma_start(out=sbm[:], in_=c_sbmask), writes=[sbm], dma=True)
                S_.op("pool", lambda e: e.memset(zeros_bf[:], 0.0), writes=[zeros_bf])
                tasks = []
                it = 0
                nblk = 0
                for h in range(NH):
                    kTh = kT[h % 2]
                    qTh = qT[h % 2]
                    S_.op("sp", (lambda kTh=kTh, h=h: lambda e: e.dma_start(out=kTh[:], in_=skT[h * 64:(h + 1) * 64, :]))(), writes=[kTh], dma=True)
                    S_.op("sp", (lambda qTh=qTh, h=h: lambda e: e.dma_start(out=qTh[:], in_=sqT[h * 64:(h + 1) * 64, :]))(), writes=[qTh], dma=True)
                    for bi in range(NT):
                        t0 = bi * 512
                        osbt = osb[nblk % 2]
                        nblk += 1
                        nkt = bi * 4 + 4
                        items = [(kt,) for kt in range(nkt - 1, bi * 4 - 1, -1)] + [(kt, kt - 1) for kt in range(bi * 4 - 1, 0, -2)]
                        for ii, kts in enumerate(items):
                            npr = len(kts)
                            diag = kts[0] - bi * 4 if npr == 1 else -1
                            c0 = 128 * diag if diag >= 0 else 0
                            first = (ii == 0)
                            lastk = (ii == len(items) - 1)
                            Eb, Lb, tb, eb, Wb = E[it % NE], L[it % 3], tmp[it % 3], eC[it % 3], Wt[it % 3]
                            pzb = pz[it % 2]
                            it += 1

                            def s0(pzb=pzb, kTh=kTh, qTh=qTh, kts=kts, c0=c0, t0=t0, S_=S_):
                                for j, kt in enumerate(kts):
                                    S_.op("pe", (lambda j=j, kt=kt: lambda e: e.matmul(pzb[:, j, c0:512], lhsT=kTh[:, kt * 128:(kt + 1) * 128], rhs=qTh[:, t0 + c0:t0 + 512], start=True, stop=True))(), reads=[kTh, qTh], writes=[pzb])

                            def s1(Eb=Eb, pzb=pzb, c0=c0, npr=npr, S_=S_):
                                S_.op("act", lambda e: e.activation(out=Eb[:, 0:npr, c0:512], in_=pzb[:, 0:npr, c0:512], func=AF.Exp), reads=[pzb], writes=[Eb])

                            def s2(Eb=Eb, Lb=Lb, c0=c0, npr=npr, S_=S_):
                                S_.op("act", lambda e: e.activation(out=Lb[:, 0:npr, c0:512], in_=Eb[:, 0:npr, c0:512], func=AF.Ln, bias=1.0, scale=1.0), reads=[Eb], writes=[Lb])

                            def s3(Eb=Eb, Lb=Lb, c0=c0, diag=diag, npr=npr, S_=S_):
                                if diag >= 0:
                                    S_.op("dve", lambda e: e.tensor_tensor(out=Eb[:, 0, c0:c0 + 128], in0=Eb[:, 0, c0:c0 + 128], in1=sbm[:], op=ALU.mult), reads=[Eb, sbm], writes=[Eb])
                                    S_.op("dve", lambda e: e.tensor_tensor(out=Lb[:, 0, c0:c0 + 128], in0=Lb[:, 0, c0:c0 + 128], in1=sbm[:], op=ALU.mult), reads=[Lb, sbm], writes=[Lb])
                                S_.op("pe", lambda e: e.matmul(pc[:, 0, c0:512], lhsT=tri[:], rhs=Lb[:, 0, c0:512], start=True, stop=True), reads=[tri, Lb], writes=[pc])
                                if npr == 2:
                                    S_.op("pe", lambda e: e.matmul(pc[:, 1, :], lhsT=tri[:], rhs=Lb[:, 1, :], start=True, stop=False), reads=[tri, Lb], writes=[pc])
                                    S_.op("pe", lambda e: e.matmul(pc[:, 1, :], lhsT=ones_bf[:], rhs=Lb[:, 0, :], start=False, stop=True), reads=[ones_bf, Lb], writes=[pc])
                                    S_.op("pe", lambda e: e.matmul(pb[:], lhsT=ones_bf[:], rhs=Lb[:, 0, :], start=True, stop=False), reads=[ones_bf, Lb], writes=[pb])
                                    S_.op("pe", lambda e: e.matmul(pb[:], lhsT=ones_bf[:], rhs=Lb[:, 1, :], start=False, stop=True), reads=[ones_bf, Lb], writes=[pb])
                                else:
                                    S_.op("pe", lambda e: e.matmul(pb[:, c0:512], lhsT=ones_bf[:], rhs=Lb[:, 0, c0:512], start=True, stop=True), reads=[ones_bf, Lb], writes=[pb])

                            def s4(tb=tb, c0=c0, first=first, npr=npr, S_=S_):
                                if first:
                                    S_.op("pool", lambda e: e.memset(carry[:], 0.0), writes=[carry])
                                for j in range(npr):
                                    S_.op("dve", (lambda j=j: lambda e: e.tensor_tensor(out=tb[:, j, c0:512], in0=pc[:, j, c0:512], in1=carry[:, c0:512], op=ALU.add))(), reads=[pc, carry], writes=[tb])
                                S_.op("dve", lambda e: e.tensor_tensor(out=carry[:, c0:512], in0=pb[:, c0:512], in1=carry[:, c0:512], op=ALU.add), reads=[pb, carry], writes=[carry])

                            def s5(eb=eb, tb=tb, c0=c0, npr=npr, S_=S_):
                                S_.op("act", lambda e: e.activation(out=eb[:, 0:npr, c0:512], in_=tb[:, 0:npr, c0:512], func=AF.Exp, scale=-1.0), reads=[tb], writes=[eb])

                            def s6(Wb=Wb, Eb=Eb, eb=eb, c0=c0, npr=npr, S_=S_):
                                S_.op("pool", lambda e: e.tensor_tensor(out=Wb[:, 0:npr, c0:512], in0=Eb[:, 0:npr, c0:512], in1=eb[:, 0:npr, c0:512], op=ALU.mult), reads=[Eb, eb], writes=[Wb])

                            def s7(Wb=Wb, kts=kts, h=h, c0=c0, first=first, lastk=lastk, qTh=qTh, t0=t0, S_=S_):
                                if first:
                                    S_.op("pe", lambda e: e.matmul(pot[0:64, :], lhsT=zeros_bf[0:64, :], rhs=qTh[:, t0:t0 + 512], start=True, stop=False), reads=[zeros_bf, qTh], writes=[pot])
                                for j, kt in enumerate(kts):
                                    S_.op("pe", (lambda j=j, kt=kt: lambda e: e.matmul(pot[0:64, c0:512], lhsT=Vs[:, kt, h * 64:(h + 1) * 64], rhs=Wb[:, j, c0:512], start=False, stop=(lastk and j == len(kts) - 1)))(), reads=[Wb], writes=[pot])

                            stages = [s0, s1, s2, s3, s4, s5, s6, s7]
                            if lastk:
                                def s8(osbt=osbt, S_=S_):
                                    S_.op("act", lambda e: e.copy(out=osbt[:], in_=pot[0:64, :]), reads=[pot], writes=[osbt])

                                def s9(osbt=osbt, h=h, t0=t0, S_=S_):
                                    S_.op("sp", lambda e: e.dma_start(out=mixo_t[4 + h].ap()[:, MOFF + t0:MOFF + t0 + 512], in_=osbt[:]), reads=[osbt], writes=[("mixed_b", h, t0)], dma=True, stream=osbt)
                                stages += [s8, s9]
                            tasks.append(stages)
                run_pipeline(tasks)
                S_.emit()

            with ExitStack() as st2:
                def sb2(name, shape, dt):
                    return T(st2.enter_context(nc.sbuf_tensor(uniq(name), list(shape), dt)), name)

                def ps2(name, shape=(128, 512), dt=F32):
                    return T(st2.enter_context(nc.psum_tensor(uniq(name), list(shape), dt)), name)

                S_ = Sched(ctx)
                Ka = [sb2("Ka%d" % i, [128, S], BF16) for i in range(2)]
                Qa = [sb2("Qa%d" % i, [128, S], BF16) for i in range(2)]
                dfm = sb2("dfm", [128, NH, 128], F32)
                NP = 5
                P = [sb2("P%d" % i, [128, 2, 512], BF16) for i in range(NP)]
                lamv = sb2("lamv", [128, 4, 32], F32)
                lamp = sb2("lamp", [128, 2, 32], F32)
                lams = sb2("lams", [128, 4], F32)
                ones_f = sb2("ones_f", [128, 64], BF16)
                gs = sb2("gs", [64, 1], F32)
                pvs = [[sb2("pvs%d_%d" % (i, m), [128, 512], F32) for m in range(2)] for i in range(2)]
                rden = [sb2("rden%d" % i, [128, 2, 512], F32) for i in range(2)]
                rh = [sb2("rh%d" % i, [128, 2, 512], BF16) for i in range(2)]
                rl = [sb2("rl%d" % i, [128, 2, 512], BF16) for i in range(2)]
                rb = [[sb2("rb%d_%d" % (i, m), [64, 512], F32) for m in range(2)] for i in range(2)]
                od = [sb2("od%d" % i, [64, 512], F32) for i in range(2)]
                od2 = [sb2("od2%d" % i, [64, 512], F32) for i in range(2)]
                osq = [sb2("osq%d" % i, [64, 512], BF16) for i in range(2)]
                rs = [sb2("rs%d" % i, [64, 512], F32) for i in range(2)]
                oc = [sb2("oc%d" % i, [64, 512], BF16) for i in range(2)]
                psc = [ps2("psc%d" % i, (128, 2, 512)) for i in range(2)]
                pov = [ps2("pov%d" % m) for m in range(2)]
                pbc = [ps2("pbc%d" % i) for i in range(2)]

                S_.op("sp", lambda e: e.dma_start(out=dfm[:], in_=c_dfmask.rearrange("h k q -> k h q")), writes=[dfm], dma=True)
                S_.op("pool", lambda e: e.memset(ones_f[:], 1.0), writes=[ones_f])
                for i in range(2):
                    for m in range(2):
                        S_.op("pool", (lambda i=i: lambda e: e.memset(Ka[i][:], 0.0))() if m == 0 else (lambda i=i: lambda e: e.memset(Qa[i][:], 0.0))(), writes=[(Ka[i], mm, kk) for mm in range(2) for kk in range(2)] if m == 0 else [(Qa[i], mm, kk) for mm in range(2) for kk in range(2)])
                S_.op("sp", lambda e: e.dma_start(out=gs[:], in_=subln_g[l].rearrange("(d o) -> d o", o=1)), writes=[gs], dma=True)
                S_.op("dve", lambda e: e.tensor_scalar(out=gs[:], in0=gs[:], scalar1=float(1.0 - lam_init), scalar2=None, op0=ALU.mult), reads=[gs], writes=[gs])
                lamkeys = [(lamv, i) for i in range(4)]
                for i, v in enumerate((lam_q1, lam_k1, lam_q2, lam_k2)):
                    S_.op("sp", (lambda i=i, v=v: lambda e: e.dma_start(out=lamv[64:65, i, :], in_=v[l:l + 1, :]))(), writes=[(lamv, i)], dma=True, stream=("lamv", 0))
                S_.op("dve", lambda e: e.tensor_tensor(out=lamp[64:65, 0, :], in0=lamv[64:65, 0, :], in1=lamv[64:65, 1, :], op=ALU.mult), reads=lamkeys, writes=[(lamp, 0)])
                S_.op("dve", lambda e: e.tensor_tensor(out=lamp[64:65, 1, :], in0=lamv[64:65, 2, :], in1=lamv[64:65, 3, :], op=ALU.mult), reads=lamkeys, writes=[(lamp, 1)])
                S_.op("dve", lambda e: e.reduce_sum(out=lams[64:65, 0:1], in_=lamp[64:65, 0, :], axis=mybir.AxisListType.X), reads=[(lamp, 0)], writes=[(lams, 0)])
                S_.op("dve", lambda e: e.reduce_sum(out=lams[64:65, 1:2], in_=lamp[64:65, 1, :], axis=mybir.AxisListType.X), reads=[(lamp, 1)], writes=[(lams, 1)])
                S_.op("act", lambda e: e.activation(out=lams[64:65, 0:2], in_=lams[64:65, 0:2], func=AF.Exp), reads=[(lams, 0), (lams, 1)], writes=[(lams, 2)])
                S_.op("dve", lambda e: e.tensor_tensor(out=lams[64:65, 2:3], in0=lams[64:65, 1:2], in1=lams[64:65, 0:1], op=ALU.subtract), reads=[(lams, 2)], writes=[(lams, 3)])
                S_.op("dve", lambda e: e.tensor_scalar(out=lams[64:65, 3:4], in0=lams[64:65, 2:3], scalar1=float(-lam_init), scalar2=None, op0=ALU.add), reads=[(lams, 3)], writes=[(lams, 4)])
                neglam = lams[64:65, 3:4]

                def load_dhead(Kah, Qah, h, S_=S_):
                    for m in range(2):
                        S_.op("sp", (lambda m=m: lambda e: e.dma_start(out=Kah[m * 64:m * 64 + 32, :], in_=dkT[h, m]))(), writes=[(Kah, m, 0)], dma=True, stream=(Kah, 0))
                        S_.op("sp", (lambda m=m: lambda e: e.dma_start(out=Kah[m * 64 + 32:m * 64 + 36, :], in_=c_kaug[h]))(), writes=[(Kah, m, 1)], dma=True, stream=(Kah, 0))
                        S_.op("sp", (lambda m=m: lambda e: e.dma_start(out=Qah[m * 64:m * 64 + 32, :], in_=dqT[h, m]))(), writes=[(Qah, m, 0)], dma=True, stream=(Qah, 0))
                        S_.op("sp", (lambda m=m: lambda e: e.dma_start(out=Qah[m * 64 + 32:m * 64 + 36, :], in_=c_qaug[h]))(), writes=[(Qah, m, 1)], dma=True, stream=(Qah, 0))

                tasks = []
                it = 0
                nblk = 0
                for h in range(NH):
                    Kah = Ka[h % 2]
                    Qah = Qa[h % 2]
                    if h < 2:
                        load_dhead(Kah, Qah, h)
                    kqkeys = [(Kah, 0, 0), (Kah, 0, 1), (Kah, 1, 0), (Kah, 1, 1), (Qah, 0, 0), (Qah, 0, 1), (Qah, 1, 0), (Qah, 1, 1)]
                    for bi in range(NT):
                        t0 = bi * 512
                        par = nblk % 2
                        nblk += 1
                        nkt = bi * 4 + 4
                        for kt in range(nkt):
                            diag = kt - bi * 4
                            c0 = 128 * diag if diag >= 0 else 0
                            Pb = P[it % NP]
                            pscb = psc[it % 2]
                            it += 1

                            def s0(pscb=pscb, Kah=Kah, Qah=Qah, kt=kt, c0=c0, t0=t0, kqkeys=kqkeys, S_=S_):
                                for m in range(2):
                                    p0 = m * 64
                                    S_.op("pe", (lambda m=m, p0=p0: lambda e: e.matmul(pscb[:, m, c0:512], lhsT=Kah[p0:p0 + 64, kt * 128:(kt + 1) * 128], rhs=Qah[p0:p0 + 64, t0 + c0:t0 + 512], start=True, stop=True))(), reads=kqkeys, writes=[(pscb, m)])

                            def s1(Pb=Pb, pscb=pscb, c0=c0, S_=S_):
                                S_.op("act", lambda e: e.activation(out=Pb[:, :, c0:512], in_=pscb[:, :, c0:512], func=AF.Exp), reads=[(pscb, 0), (pscb, 1)], writes=[(Pb, 0), (Pb, 1)])

                            def s2(Pb=Pb, c0=c0, h=h, diag=diag, S_=S_):
                                if diag >= 0:
                                    for m in range(2):
                                        S_.op("dve", (lambda m=m: lambda e: e.tensor_tensor(out=Pb[:, m, c0:c0 + 128], in0=Pb[:, m, c0:c0 + 128], in1=dfm[:, h, :], op=ALU.mult))(), reads=[(Pb, m), dfm], writes=[(Pb, m)])

                            def s3(Pb=Pb, kt=kt, h=h, c0=c0, nkt=nkt, S_=S_):
                                for m in range(2):
                                    S_.op("pe", (lambda m=m: lambda e: e.matmul(pov[m][:, c0:512], lhsT=Vd[:, kt, h * 65:h * 65 + 128], rhs=Pb[:, m, c0:512], start=(kt == 0), stop=(kt == nkt - 1)))(), reads=[(Pb, m)], writes=[pov[m]])

                            stages = [s0, s1, s2, s3]
                            if kt == nkt - 1:
                                pv0, pv1 = pvs[par][0], pvs[par][1]
                                rd, rh_, rl_, rb_, od_, od2_, osq_, rs_, oc_ = rden[par], rh[par], rl[par], rb[par], od[par], od2[par], osq[par], rs[par], oc[par]

                                def ec(pv0=pv0, pv1=pv1, S_=S_):
                                    S_.op("act", lambda e: e.copy(out=pv0[0:65, :], in_=pov[0][0:65, :]), reads=[pov[0]], writes=[pv0])
                                    S_.op("act", lambda e: e.copy(out=pv1[0:65, :], in_=pov[1][0:65, :]), reads=[pov[1]], writes=[pv1])

                                def e0(pv0=pv0, pv1=pv1, rd=rd, S_=S_):
                                    S_.op("dve", lambda e: e.reciprocal(out=rd[64:65, 0, :], in_=pv0[64:65, :]), reads=[pv0], writes=[(rd, 0)])
                                    S_.op("dve", lambda e: e.reciprocal(out=rd[64:65, 1, :], in_=pv1[64:65, :]), reads=[pv1], writes=[(rd, 1)])

                                def e1(rd=rd, S_=S_):
                                    S_.op("dve", lambda e: e.tensor_scalar(out=rd[64:65, 1, :], in0=rd[64:65, 1, :], scalar1=neglam, scalar2=None, op0=ALU.mult), reads=[(rd, 1), (lams, 4)], writes=[(rd, 1)])

                                def e2(rd=rd, rh_=rh_, S_=S_):
                                    S_.op("dve", lambda e: e.tensor_copy(out=rh_[64:65, :, :], in_=rd[64:65, :, :]), reads=[(rd, 0), (rd, 1)], writes=[rh_])

                                def e3(rd=rd, rh_=rh_, rl_=rl_, S_=S_):
                                    S_.op("dve", lambda e: e.tensor_tensor(out=rl_[64:65, :, :], in0=rd[64:65, :, :], in1=rh_[64:65, :, :], op=ALU.subtract), reads=[(rd, 0), (rd, 1), rh_], writes=[rl_])

                                def e4(rh_=rh_, rl_=rl_, S_=S_):
                                    for mm in range(2):
                                        S_.op("pe", (lambda mm=mm: lambda e: e.matmul(pbc[mm][0:64, :], lhsT=ones_f[64:65, :], rhs=rh_[64:65, mm, :], start=True, stop=False))(), reads=[ones_f, rh_], writes=[pbc[mm]])
                                        S_.op("pe", (lambda mm=mm: lambda e: e.matmul(pbc[mm][0:64, :], lhsT=ones_f[64:65, :], rhs=rl_[64:65, mm, :], start=False, stop=True))(), reads=[ones_f, rl_], writes=[pbc[mm]])

                                def e5(rb_=rb_, S_=S_):
                                    for mm in range(2):
                                        S_.op("act", (lambda mm=mm: lambda e: e.copy(out=rb_[mm][:], in_=pbc[mm][0:64, :]))(), reads=[pbc[mm]], writes=[rb_[mm]])

                                def e6(pv0=pv0, pv1=pv1, rb_=rb_, od_=od_, od2_=od2_, S_=S_):
                                    S_.op("dve", lambda e: e.tensor_tensor(out=od_[:], in0=pv0[0:64, :], in1=rb_[0][:], op=ALU.mult), reads=[pv0, rb_[0]], writes=[od_])
                                    S_.op("dve", lambda e: e.tensor_tensor(out=od2_[:], in0=pv1[0:64, :], in1=rb_[1][:], op=ALU.mult), reads=[pv1, rb_[1]], writes=[od2_])

                                def e7(od_=od_, od2_=od2_, S_=S_):
                                    S_.op("dve", lambda e: e.tensor_tensor(out=od_[:], in0=od_[:], in1=od2_[:], op=ALU.add), reads=[od_, od2_], writes=[od_])

                                def e8(od_=od_, osq_=osq_, S_=S_):
                                    S_.op("pool", lambda e: e.tensor_tensor(out=osq_[:], in0=od_[:], in1=od_[:], op=ALU.mult), reads=[od_], writes=[osq_])

                                def e9(osq_=osq_, S_=S_):
                                    S_.op("pe", lambda e: e.matmul(pbc[0][0:64, :], lhsT=ones_bf[0:64, 0:64], rhs=osq_[:], start=True, stop=True), reads=[ones_bf, osq_], writes=[pbc[0]])

                                def e10(rs_=rs_, S_=S_):
                                    S_.op("act", lambda e: e.activation(out=rs_[:], in_=pbc[0][0:64, :], func=AF.Sqrt, bias=EPS, scale=1.0 / 64), reads=[pbc[0]], writes=[rs_])

                                def e11(rs_=rs_, S_=S_):
                                    S_.op("dve", lambda e: e.reciprocal(out=rs_[:], in_=rs_[:]), reads=[rs_], writes=[rs_])

                                def e12(od_=od_, rs_=rs_, oc_=oc_, S_=S_):
                                    S_.op("dve", lambda e: e.scalar_tensor_tensor(out=oc_[:], in0=od_[:], scalar=gs[:, 0:1], in1=rs_[:], op0=ALU.mult, op1=ALU.mult), reads=[od_, gs, rs_], writes=[oc_])

                                def e13(oc_=oc_, h=h, t0=t0, S_=S_):
                                    S_.op("sp", lambda e: e.dma_start(out=mixo_t[6 + h].ap()[:, MOFF + t0:MOFF + t0 + 512], in_=oc_[:]), reads=[oc_], writes=[("mixed_c", h, t0)], dma=True, stream=oc_)
                                def e45(e4=e4, e5=e5):
                                    e4(); e5()

                                def e910(e9=e9, e10=e10):
                                    e9(); e10()
                                stages += [ec, e0, e1, e2, e3, e45, e6, e7, e8, e910, e11, e12, e13]
                            tasks.append(stages)
                    if h + 2 < NH:
                        tasks.append([(lambda Kah=Kah, Qah=Qah, hn=h + 2: lambda: load_dhead(Kah, Qah, hn))()])
                run_pipeline(tasks)
                S_.emit()


        dtiles = [(-HALO, HALO)] + [(k * 512, 512) for k in range(NTD)]
        with ExitStack() as st2:
            def sb2(name, shape, dt):
                return T(st2.enter_context(nc.sbuf_tensor(uniq(name), list(shape), dt)), name)

            def ps2(name, shape=(128, 512), dt=F32):
                return T(st2.enter_context(nc.psum_tensor(uniq(name), list(shape), dt)), name)

            S_ = Sched(ctx)
            wo = sb2("wo", [128, 8, D], BF16)
            wup = sb2("wup", [128, 8, 2 * D_FF], BF16)
            wst = [sb2("wst%d" % i, [128, D_FF // 2], F32) for i in range(2)]
            g2 = sb2("g2", [128, 8], F32)
            cfw = sb2("cfw", [128, 3, 44], F32)
            cfb = sb2("cfb", [128, 44], F32)
            ones_bf = sb2("ones_bf2", [128, 128], BF16)
            xtc = [sb2("xtc%d" % i, [128, 512], F32) for i in range(2)]
            mx = [sb2("mx%d" % i, [128, 8, 512], BF16) for i in range(1)]
            sqbD = sb2("sqbD", [128, 8, 512], BF16)
            mxB = sb2("mxB", [128, 8, 512], BF16)
            jm = sb2("jmD", [128, 1], F32)
            jm1 = sb2("jm1D", [128, 1], F32)
            xm = sb2("xm", [128, 8, 512], F32)
            rstd = sb2("rstd", [128, 512], F32)
            h2 = sb2("h2", [128, 8, 512], BF16)
            sqb = h2
            uext = [sb2("uext%d" % i, [128, 2 + 512], F32) for i in range(4)]
            ucar = sb2("ucar", [128, 44, 2], F32)
            acc = [sb2("acc%d" % i, [128, 512], F32) for i in range(6)]
            sg = [sb2("sg%d" % i, [128, 512], F32) for i in range(2)]
            act_sb = [sb2("act_sb%d" % i, [128, 512], BF16) for i in range(3)]
            ppP = [ps2("ppP%d" % i) for i in range(2)]
            ppF = [ps2("ppF%d" % i) for i in range(4)]
            pss = ps2("pss")

            def gather_mixed(g):
                ins = None
                for src_t, dst_t in zip(mixo_t, mixg_t):
                    ins = g.collective_compute("AllGather", ALU.bypass, replica_groups=GROUPS,
                                               ins=[src_t.ap().opt()], outs=[dst_t.ap().opt()]).then_inc(cc_sem)
                    cc_count[0] += 1
                return ins
            S_.op("pool", gather_mixed)
            S_.op("pool", lambda e: e.memset(ones_bf[:], 1.0), writes=[ones_bf])
            S_.op("pool", lambda e: e.memset(ucar[:], 0.0), writes=[ucar])
            S_.op("sp", lambda e: e.dma_start(out=g2[:], in_=norm2_g[l]), writes=[g2], dma=True)
            S_.op("sp", lambda e: e.dma_start(out=jm[:], in_=jmask), writes=[jm], dma=True)
            S_.op("dve", lambda e: e.tensor_scalar(out=jm1[:], in0=jm[:], scalar1=-1.0, scalar2=1.0, op0=ALU.mult, op1=ALU.add), reads=[jm], writes=[jm1])
            S_.op("sp", lambda e: e.dma_start(out=cfw[:], in_=conv_ff_w[l]), writes=[cfw], dma=True)
            S_.op("sp", lambda e: e.dma_start(out=cfb[:], in_=conv_ff_b[l]), writes=[cfb], dma=True)
            nst = 0
            for kc in range(8):
                stg = wst[nst % 2]; nst += 1
                S_.op("sp", (lambda kc=kc, stg=stg: lambda e: e.dma_start(out=stg[:, 0:D], in_=w_out[l, kc * 128:(kc + 1) * 128, :]))(), writes=[stg], dma=True)
                S_.op("act" if kc % 2 else "dve", (lambda kc=kc, stg=stg: lambda e: (e.copy if kc % 2 else e.tensor_copy)(out=wo[:, kc, :], in_=stg[:, 0:D]))(), reads=[stg], writes=[(wo, kc)])
            HW = D_FF // 2
            for kc in range(8):
                for hf in range(4):
                    stg = wst[nst % 2]; nst += 1
                    S_.op("sp", (lambda kc=kc, hf=hf, stg=stg: lambda e: e.dma_start(out=stg[:], in_=w_ff_up[l, kc * 128:(kc + 1) * 128, hf * HW:(hf + 1) * HW]))(), writes=[stg], dma=True)
                    S_.op("act" if hf % 2 else "dve", (lambda kc=kc, hf=hf, stg=stg: lambda e: (e.copy if hf % 2 else e.tensor_copy)(out=wup[:, kc, hf * HW:(hf + 1) * HW], in_=stg[:]))(), reads=[stg], writes=[(wup, kc)])

            xbuf_c = chunked(xbuf)
            xh_c = chunked(xh)
            xmid_c = chunked(xmidT)
            act_c = chunked(actT)
            mxt = mx[0]
            nF = [0]

            def prologue_task(tix, rel, w):
                ldk = [(mxt, "ld", kc) for kc in range(8)] + [(mxB, "ld", kc) for kc in range(8)]
                xs_c = xh_c if l == 0 else xbuf_c

                def p0():
                    def ld_a(kc):
                        def fn(e):
                            if tix == 0 and kc == 0:
                                e.wait_ge(cc_sem, cc_count[0])
                            return e.dma_start(out=mxt[:, kc, 0:w], in_=mixg_t[kc].ap()[:, PAD + rel:PAD + rel + w])
                        return fn
                    for kc in range(8):
                        S_.op("sp", ld_a(kc), writes=[(mxt, "ld", kc), mxt] if kc == 0 else [(mxt, "ld", kc)], dma=True, stream=(mxt, "ld"))
                        S_.op("sp", (lambda kc=kc: lambda e: e.dma_start(out=mxB[:, kc, 0:w], in_=mixg_t[kc].ap()[:, PAD + HS + rel:PAD + HS + rel + w]))(), writes=[(mxB, "ld", kc)], dma=True, stream=(mxB, "ld"))

                def p1():
                    for kc in range(8):
                        S_.op("act", (lambda kc=kc: lambda e: e.activation(out=mxt[:, kc, 0:w], in_=mxt[:, kc, 0:w], func=AF.Identity, scale=jm1[:, 0:1]))(), reads=ldk + [jm1], writes=[(mxt, "a", kc)])

                def p2():
                    for kc in range(8):
                        S_.op("dve", (lambda kc=kc: lambda e: e.scalar_tensor_tensor(out=mxt[:, kc, 0:w], in0=mxB[:, kc, 0:w], scalar=jm[:, 0:1], in1=mxt[:, kc, 0:w], op0=ALU.mult, op1=ALU.add))(), reads=ldk + [(mxt, "a", kc), jm], writes=[(mxt, "s", kc)] + ([mxt] if kc == 7 else []))

                def wout(cos):
                    for co in cos:
                        pt = ppP[co % 2]
                        xt_ = xtc[co % 2]
                        S_.op("sp", (lambda co=co, xt_=xt_: lambda e: e.dma_start(out=xt_[:, 0:w], in_=xs_c[:, co, PAD + rel:PAD + rel + w]))(), writes=[xt_], dma=True)
                        for kc in range(8):
                            S_.op("pe", (lambda kc=kc, co=co, pt=pt: lambda e: e.matmul(pt[:, 0:w], lhsT=wo[:, kc, co * 128:(co + 1) * 128], rhs=mxt[:, kc, 0:w], start=(kc == 0), stop=(kc == 7)))(), reads=[(wo, kc), mxt], writes=[pt])
                        S_.op("dve", (lambda co=co, pt=pt, xt_=xt_: lambda e: e.tensor_tensor(out=xm[:, co, 0:w], in0=pt[:, 0:w], in1=xt_[:, 0:w], op=ALU.add))(), reads=[pt, xt_], writes=[(xm, co)])
                        S_.op("pool", (lambda co=co: lambda e: e.tensor_tensor(out=sqbD[:, co, 0:w], in0=xm[:, co, 0:w], in1=xm[:, co, 0:w], op=ALU.mult))(), reads=[(xm, co)], writes=[(sqbD, co)])

                def p3():
                    wout(range(0, 4))

                def p4():
                    wout(range(4, 8))

                def p5():
                    S_.op("sp", lambda e: e.dma_start(out=xmid_c[:, :, PAD + rel:PAD + rel + w], in_=xm[:, :, 0:w]), reads=[(xm, co) for co in range(8)], writes=[("xmid", tix)], dma=True, stream=xm)
                    for c in range(8):
                        S_.op("pe", (lambda c=c: lambda e: e.matmul(pss[:, 0:w], lhsT=ones_bf[:], rhs=sqbD[:, c, 0:w], start=(c == 0), stop=(c == 7)))(), reads=[ones_bf, (sqbD, c)], writes=[pss])

                def p6():
                    S_.op("act", lambda e: e.activation(out=rstd[:, 0:w], in_=pss[:, 0:w], func=AF.Sqrt, bias=EPS, scale=1.0 / D), reads=[pss], writes=[rstd])

                def p7():
                    S_.op("dve", lambda e: e.reciprocal(out=rstd[:, 0:w], in_=rstd[:, 0:w]), reads=[rstd], writes=[rstd])
                return [p0, p1, p2, p3, p4, p5, p6, p7]

            def h_task(tix, rel, w):
                def h0():
                    for c in range(8):
                        S_.op("dve", (lambda c=c: lambda e: e.scalar_tensor_tensor(out=h2[:, c, 0:w], in0=xm[:, c, 0:w], scalar=g2[:, c:c + 1], in1=rstd[:, 0:w], op0=ALU.mult, op1=ALU.mult))(), reads=[(xm, c), g2, rstd], writes=[(h2, c)])
                return [h0]

            def ffn_task(tix, rel, w, fc):
                k = nF[0]
                nF[0] += 1
                pts = [ppF[(2 * k) % 4], ppF[(2 * k + 1) % 4]]
                ues = [uext[(2 * k) % 4], uext[(2 * k + 1) % 4]]
                acs = [acc[(2 * k) % 6], acc[(2 * k + 1) % 6]]
                ccs = [fc, 22 + fc]
                sgt = sg[k % 2]
                asb = act_sb[k % 3]

                def f0():
                    for gv in range(2):
                        col0 = ccs[gv] * 128
                        for kc in range(8):
                            S_.op("pe", (lambda kc=kc, col0=col0, pt=pts[gv]: lambda e: e.matmul(pt[:, 0:w], lhsT=wup[:, kc, col0:col0 + 128], rhs=h2[:, kc, 0:w], start=(kc == 0), stop=(kc == 7)))(), reads=[(wup, kc), (h2, kc)], writes=[pts[gv]])

                def f1():
                    for gv in range(2):
                        ue, pt, ac, cc = ues[gv], pts[gv], acs[gv], ccs[gv]
                        S_.op("act", (lambda ue=ue, pt=pt: lambda e: e.copy(out=ue[:, 2:2 + w], in_=pt[:, 0:w]))(), reads=[pt], writes=[(ue, "m")])
                        S_.op("pool", (lambda ue=ue, cc=cc: lambda e: e.tensor_copy(out=ue[:, 0:2], in_=ucar[:, cc, :]))(), reads=[(ucar, cc)], writes=[(ue, "h")])
                        S_.op("act", (lambda ac=ac, pt=pt, cc=cc: lambda e: e.activation(out=ac[:, 0:w], in_=pt[:, 0:w], func=AF.Identity, bias=cfb[:, cc:cc + 1], scale=cfw[:, 2, cc:cc + 1]))(), reads=[pt, cfw, cfb], writes=[ac])

                def f2():
                    for gv in range(2):
                        ue, ac, cc = ues[gv], acs[gv], ccs[gv]
                        for kk in range(2):
                            S_.op("dve", (lambda ue=ue, ac=ac, cc=cc, kk=kk: lambda e: e.scalar_tensor_tensor(out=ac[:, 0:w], in0=ue[:, kk:kk + w], scalar=cfw[:, kk, cc:cc + 1], in1=ac[:, 0:w], op0=ALU.mult, op1=ALU.add))(), reads=[(ue, "m"), (ue, "h"), cfw, ac], writes=[ac])

                def f3():
                    for gv in range(2):
                        ue, cc = ues[gv], ccs[gv]
                        S_.op("pool", (lambda ue=ue, cc=cc: lambda e: e.tensor_copy(out=ucar[:, cc, :], in_=ue[:, w:w + 2]))(), reads=[(ue, "m"), (ue, "h")], writes=[(ucar, cc)])
                    S_.op("act", lambda e: e.activation(out=sgt[:, 0:w], in_=acs[0][:, 0:w], func=AF.Silu), reads=[acs[0]], writes=[sgt])

                def f4():
                    S_.op("pool", lambda e: e.tensor_tensor(out=asb[:, 0:w], in0=sgt[:, 0:w], in1=acs[1][:, 0:w], op=ALU.mult), reads=[sgt, acs[1]], writes=[asb])

                def f5():
                    S_.op("sp", lambda e: e.dma_start(out=act_c[:, fc, PAD + rel:PAD + rel + w], in_=asb[:, 0:w]), reads=[asb], writes=[("act", tix, fc)], dma=True, stream=asb)
                return [f0, f1, f2, f3, f4, f5]

            tasks = [prologue_task(0, *dtiles[0])] + [[] for _ in range(8)]
            for tix, (rel, w) in enumerate(dtiles):
                tasks.append(h_task(tix, rel, w))
                for fc in range(22):
                    tasks.append(ffn_task(tix, rel, w, fc))
                    if fc == 8 and tix + 1 < len(dtiles):
                        tasks.append(prologue_task(tix + 1, *dtiles[tix + 1]))
            run_pipeline(tasks)
            S_.emit()

        etiles = ([] if last else [(-HALO, HALO)]) + [(k * 512, 512) for k in range(NTD)]
        with ExitStack() as st2:
            def sb2(name, shape, dt):
                return T(st2.enter_context(nc.sbuf_tensor(uniq(name), list(shape), dt)), name)

            def ps2(name, shape=(128, 512), dt=F32):
                return T(st2.enter_context(nc.psum_tensor(uniq(name), list(shape), dt)), name)

            S_ = Sched(ctx)
            wd = sb2("wd", [128, 22, D], BF16)
            wst = [sb2("wst%d" % i, [128, D], F32) for i in range(2)]
            gf = sb2("gf", [128, 8], F32)
            g1n = sb2("g1n", [128, 8], F32)
            hn = [sb2("hn%d" % i, [128, 8, 512], BF16) for i in range(2)]
            jm = sb2("jm", [128, 1], F32)
            ones_bf = sb2("ones_bf3", [128, 128], BF16)
            at = [sb2("at%d" % i, [128, 22, 512], BF16) for i in range(2)]
            xm = [sb2("xm%d" % i, [128, 8, 512], F32) for i in range(2)]
            xn = [sb2("xn%d" % i, [128, 8, 512], F32) for i in range(2)]
            sqb = sb2("sqb", [128, 8, 512], BF16)
            rstd = sb2("rstd", [128, 512], F32)
            pp = [ps2("pp%d" % i) for i in range(6)]
            pss = ps2("pss")
            S_.op("pool", lambda e: e.memset(ones_bf[:], 1.0), writes=[ones_bf])
            S_.op("sp", lambda e: e.dma_start(out=gf[:], in_=final_g), writes=[gf], dma=True)
            if not last:
                S_.op("sp", lambda e: e.dma_start(out=g1n[:], in_=norm1_g[l + 1]), writes=[g1n], dma=True)
            S_.op("sp", lambda e: e.dma_start(out=jm[:], in_=jmask), writes=[jm], dma=True)
            for kc in range(22):
                stg = wst[kc % 2]
                S_.op("act" if kc % 2 else "sp", (lambda kc=kc, stg=stg: lambda e: e.dma_start(out=stg[:], in_=w_ff_down[l, kc * 128:(kc + 1) * 128, :]))(), writes=[stg], dma=True)
                S_.op("act" if kc % 2 else "dve", (lambda kc=kc, stg=stg: lambda e: (e.copy if kc % 2 else e.tensor_copy)(out=wd[:, kc, :], in_=stg[:]))(), reads=[stg], writes=[(wd, kc)])
            act_c = chunked(actT)
            xmid_c = chunked(xmidT)
            xbuf_c = chunked(xbuf)
            out_c = chunked(outT)
            npp = 0
            def e_loads(tix):
                rel_, w_ = etiles[tix]
                att_, xmt_ = at[tix % 2], xm[tix % 2]
                S_.op("sp", lambda e: e.dma_start(out=att_[:, :, 0:w_], in_=act_c[:, :, PAD + rel_:PAD + rel_ + w_]), writes=[att_], dma=True)
                S_.op("sp", lambda e: e.dma_start(out=xmt_[:, :, 0:w_], in_=xmid_c[:, :, PAD + rel_:PAD + rel_ + w_]), writes=[xmt_], dma=True)

            e_loads(0)
            for tix, (rel, w) in enumerate(etiles):
                att = at[tix % 2]
                xmt = xm[tix % 2]
                xnt = xn[tix % 2]
                if tix + 1 < len(etiles):
                    e_loads(tix + 1)
                for co in range(8):
                    pt = pp[npp % 6]; npp += 1
                    for kc in range(22):
                        S_.op("pe", (lambda kc=kc, co=co, pt=pt, att=att, w=w: lambda e: e.matmul(pt[:, 0:w], lhsT=wd[:, kc, co * 128:(co + 1) * 128], rhs=att[:, kc, 0:w], start=(kc == 0), stop=(kc == 21)))(), reads=[(wd, kc), att], writes=[pt])
                    S_.op("dve", (lambda co=co, pt=pt, xmt=xmt, xnt=xnt, w=w: lambda e: e.tensor_tensor(out=xnt[:, co, 0:w], in0=pt[:, 0:w], in1=xmt[:, co, 0:w], op=ALU.add))(), reads=[pt, xmt], writes=[(xnt, co)])
                    if last or rel >= 0:
                        S_.op("pool", (lambda co=co, xnt=xnt, w=w: lambda e: e.tensor_tensor(out=sqb[:, co, 0:w], in0=xnt[:, co, 0:w], in1=xnt[:, co, 0:w], op=ALU.mult))(), reads=[(xnt, co)], writes=[(sqb, co)])
                    elif rel < 0:
                        S_.op("dve", (lambda co=co, xnt=xnt, w=w: lambda e: e.tensor_scalar(out=xnt[:, co, 0:w], in0=xnt[:, co, 0:w], scalar1=jm[:, 0:1], scalar2=None, op0=ALU.mult))(), reads=[(xnt, co), jm], writes=[(xnt, co)])
                xnkeys = [(xnt, c) for c in range(8)]
                if last:
                    for c in range(8):
                        S_.op("pe", (lambda c=c, w=w: lambda e: e.matmul(pss[:, 0:w], lhsT=ones_bf[:], rhs=sqb[:, c, 0:w], start=(c == 0), stop=(c == 7)))(), reads=[ones_bf, (sqb, c)], writes=[pss])
                    S_.op("act", (lambda w=w: lambda e: e.activation(out=rstd[:, 0:w], in_=pss[:, 0:w], func=AF.Sqrt, bias=EPS, scale=1.0 / D))(), reads=[pss], writes=[rstd])
                    S_.op("dve", (lambda w=w: lambda e: e.reciprocal(out=rstd[:, 0:w], in_=rstd[:, 0:w]))(), reads=[rstd], writes=[rstd])
                    for c in range(8):
                        S_.op("dve", (lambda c=c, xnt=xnt, w=w: lambda e: e.scalar_tensor_tensor(out=xnt[:, c, 0:w], in0=xnt[:, c, 0:w], scalar=gf[:, c:c + 1], in1=rstd[:, 0:w], op0=ALU.mult, op1=ALU.mult))(), reads=[(xnt, c), gf, rstd], writes=[(xnt, c)])
                    S_.op("sp", (lambda rel=rel, w=w, xnt=xnt: lambda e: e.dma_start(out=out_c[:, :, rel:rel + w], in_=xnt[:, :, 0:w]))(), reads=xnkeys, writes=[("xout", tix)], dma=True, stream=xnt)
                else:
                    S_.op("sp", (lambda rel=rel, w=w, xnt=xnt: lambda e: e.dma_start(out=xbuf_c[:, :, PAD + rel:PAD + rel + w], in_=xnt[:, :, 0:w]))(), reads=xnkeys, writes=[("xout", tix)], dma=True, stream=xnt)
                    if rel >= 0:
                        hnt = hn[tix % 2]
                        for c in range(8):
                            S_.op("pe", (lambda c=c, w=w: lambda e: e.matmul(pss[:, 0:w], lhsT=ones_bf[:], rhs=sqb[:, c, 0:w], start=(c == 0), stop=(c == 7)))(), reads=[ones_bf, (sqb, c)], writes=[pss])
                        S_.op("act", (lambda w=w: lambda e: e.activation(out=rstd[:, 0:w], in_=pss[:, 0:w], func=AF.Sqrt, bias=EPS, scale=1.0 / D))(), reads=[pss], writes=[rstd])
                        S_.op("dve", (lambda w=w: lambda e: e.reciprocal(out=rstd[:, 0:w], in_=rstd[:, 0:w]))(), reads=[rstd], writes=[rstd])
                        for c in range(8):
                            S_.op("dve", (lambda c=c, xnt=xnt, hnt=hnt, w=w: lambda e: e.scalar_tensor_tensor(out=hnt[:, c, 0:w], in0=xnt[:, c, 0:w], scalar=g1n[:, c:c + 1], in1=rstd[:, 0:w], op0=ALU.mult, op1=ALU.mult))(), reads=[(xnt, c), g1n, rstd], writes=[(hnt, c)])
                        for c in range(8):
                            S_.op("sp", (lambda rel=rel, w=w, hnt=hnt, c=c: lambda e: e.dma_start(out=hmain_t[c // 2].ap()[(c % 2) * 128:(c % 2 + 1) * 128, rel:rel + w], in_=hnt[:, c, 0:w]))(), reads=[(hnt, cc) for cc in range(8)], writes=[("hmain", tix, c)], dma=True, stream=hnt)
            S_.emit()
        if not last:
            pass

    top.close()
    return nc


def make_consts(S, heads):
    bf = ml_dtypes.bfloat16
    k = np.arange(128)
    tri = (k[:, None] >= k[None, :]).astype(np.float32).astype(bf)
    sbmask = (k[:, None] < k[None, :]).astype(np.float32)
    slopes_all = 2.0 ** (-8.0 * (np.arange(4) + 1.0) / 4)
    nh = len(heads)
    dfmask = np.zeros((nh, 128, 128), np.float32)
    kk = k[:, None]; qq = k[None, :]
    allowed = (kk // 64) <= (qq // 64)
    pos = np.arange(S)
    qaug = np.zeros((nh, 4, S), np.float32)
    kaug = np.zeros((nh, 4, S), np.float32)
    for i, h in enumerate(heads):
        sl = slopes_all[h]
        v = np.where(kk <= qq, 1.0, np.exp(-2.0 * sl * (kk - qq)))
        dfmask[i] = np.where(allowed, v, 0.0)
        qaug[i, 0] = 1.0; qaug[i, 1] = 1.0
        qaug[i, 2] = -sl * (pos % 128); qaug[i, 3] = -sl * 128 * (pos // 128)
        kaug[i, 0] = sl * (pos % 128); kaug[i, 1] = sl * 128 * (pos // 128)
        kaug[i, 2] = 1.0; kaug[i, 3] = 1.0
    return {"c_tri": tri, "c_sbmask": sbmask, "c_dfmask": dfmask,
            "c_qaug": qaug.astype(bf), "c_kaug": kaug.astype(bf)}


_WNAMES = ["norm1_g", "w_in", "conv_a_w", "conv_a_b", "w_rgate", "b_rgate", "w_igate", "b_igate",
           "lru_lambda", "lam_q1", "lam_k1", "lam_q2", "lam_k2", "subln_g", "w_out", "norm2_g",
           "w_ff_up", "conv_ff_w", "conv_ff_b", "w_ff_down", "final_g"]


def prep_params(inputs, j):
    p = {n: np.asarray(inputs[n], dtype=np.float32) for n in _WNAMES}
    NL = p["w_in"].shape[0]

    def pc(a):
        sh = a.shape
        return np.ascontiguousarray(np.swapaxes(a.reshape(sh[:-1] + (sh[-1] // 128, 128)), -1, -2))

    a0 = j * 256
    cols = np.concatenate([np.arange(a0, a0 + 256), 512 + np.arange(a0, a0 + 256)] +
                          [base + j * 128 + np.arange(128) for base in (1024, 1280, 1536, 1792, 2048, 2304)])
    p["w_in"] = p["w_in"][:, :, cols]
    for n in ("conv_a_b", "b_rgate", "b_igate", "lru_lambda"):
        p[n] = pc(p[n][:, a0:a0 + 256])
    p["conv_a_w"] = p["conv_a_w"][:, :, a0:a0 + 256]
    p["w_rgate"] = p["w_rgate"][:, 4 * j:4 * j + 4]
    p["w_igate"] = p["w_igate"][:, 4 * j:4 * j + 4]
    for n in ("norm1_g", "norm2_g", "conv_ff_b", "final_g"):
        p[n] = pc(p[n])
    for n in ("conv_a_w", "conv_ff_w"):
        a = p[n]
        k = a.shape[1]
        p[n] = np.ascontiguousarray(a.reshape(NL, k, -1, 128).transpose(0, 3, 1, 2))
    own = [np.concatenate([r * 256 + np.arange(256), 512 + r * 128 + np.arange(128), 768 + r * 128 + np.arange(128)]) for r in range(2)]
    rows = np.concatenate([own[r][i * 64:(i + 1) * 64] for i in range(8) for r in range(2)])
    p["w_out"] = p["w_out"][:, rows, :]
    return {n: np.ascontiguousarray(v) for n, v in p.items()}


def kernel(**inputs):
    x = np.asarray(inputs["x"], dtype=np.float32)
    B, S, _ = x.shape
    NL = int(np.asarray(inputs["w_in"]).shape[0])
    PAD = 64
    HS = S // 2
    nc = build_program(S, NL)
    n_cores = 8
    per_j = []
    for j in range(2):
        m = prep_params(inputs, j)
        m.update(make_consts(S, [2 * j, 2 * j + 1]))
        m["jmask"] = np.full((128, 1), float(j), np.float32)
        per_j.append(m)
    in_maps = []
    for c in range(n_cores):
        b, j = (c // 2) % B, c % 2
        m = dict(per_j[j])
        xt = np.ascontiguousarray(x[b].T)
        m["xT"] = xt
        xhalf = np.zeros((D, PAD + HS), np.float32)
        lo = j * HS
        npre = min(PAD, lo)
        xhalf[:, PAD - npre:] = xt[:, lo - npre:lo + HS]
        m["xh"] = xhalf
        in_maps.append(m)
    res = run_bass_kernel_spmd(nc, in_maps, core_ids=list(range(n_cores)))
    out = np.empty((B, S, D), np.float32)
    for b in range(B):
        for j in range(2):
            out[b, j * HS:(j + 1) * HS, :] = res.results[2 * b + j]["outT"].T
    return out
```

```python
import math
import numpy as np
import ml_dtypes
from contextlib import ExitStack
import concourse.bass as bass
import concourse.mybir as mybir
from concourse.bass_utils import run_bass_kernel_spmd

F32 = mybir.dt.float32
BF16 = mybir.dt.bfloat16
AF = mybir.ActivationFunctionType
ALU = mybir.AluOpType

D = 1024
W_LRU = 512
P_IN = 2560
D_FF = 2816
EPS = 1e-6
SAME_ENGINE_RAW_SYNC = True


class _Op:
    __slots__ = ("eng", "fn", "deps", "is_dma", "stream", "marked", "ordinal", "sord")

    def __init__(self, eng, fn, is_dma, stream):
        self.eng = eng
        self.fn = fn
        self.deps = []
        self.is_dma = is_dma
        self.stream = stream
        self.marked = False
        self.ordinal = 0
        self.sord = 0


class T:
    def __init__(self, h, name):
        self.h = h
        self.name = name

    def __getitem__(self, idx):
        return self.h[idx]


class Ctx:
    ENGS = ("pe", "act", "dve", "pool", "sp")

    def __init__(self, nc, stack, n_dma_sems=64):
        self.nc = nc
        self.eng_sem = {e: stack.enter_context(nc.semaphore("s_" + e)) for e in self.ENGS}
        self.dma_sems = [stack.enter_context(nc.semaphore("d%d" % i)) for i in range(n_dma_sems)]
        self.first = True

    def clear_block(self):
        nc = self.nc
        with nc.Block() as block:
            @block.sync
            def _(eng):
                for s in list(self.eng_sem.values()) + self.dma_sems:
                    eng.sem_clear(s)


class Sched:
    ENGS = Ctx.ENGS

    def __init__(self, ctx):
        self.ctx = ctx
        self.nc = ctx.nc
        self.ops = {e: [] for e in self.ENGS}
        self.state = {}
        self.stream_cnt = {}
        self.stream_last = {}
        self.nops = 0

    def op(self, eng, fn, reads=(), writes=(), dma=False, stream=None):
        if dma and stream is None:
            stream = writes[0] if len(writes) else reads[0]
        o = _Op(eng, fn, dma, stream)
        deps = {}
        for k in reads:
            st = self.state.get(k)
            if st is not None and st[0] is not None:
                deps[id(st[0])] = (st[0], True)
        for k in writes:
            st = self.state.get(k)
            if st is not None:
                if st[0] is not None and id(st[0]) not in deps:
                    deps[id(st[0])] = (st[0], False)
                for r in st[1]:
                    if id(r) not in deps:
                        deps[id(r)] = (r, False)
        for k in reads:
            st = self.state.get(k)
            if st is None:
                st = [None, []]
                self.state[k] = st
            st[1].append(o)
        for k in writes:
            self.state[k] = [o, []]
        for d, raw in deps.values():
            if d is o:
                continue
            need = True
            if (not d.is_dma) and (not o.is_dma) and d.eng == o.eng:
                need = raw and SAME_ENGINE_RAW_SYNC and o.eng != "pe"
            if need:
                o.deps.append(d)
                d.marked = True
        if dma:
            c = self.stream_cnt.get(stream, 0) + 1
            self.stream_cnt[stream] = c
            o.sord = c
            self.stream_last[stream] = o
        self.ops[eng].append(o)
        self.nops += 1
        return o

    def emit(self):
        ctx = self.ctx
        nc = self.nc
        if not ctx.first:
            ctx.clear_block()
        ctx.first = False
        eng_sem = ctx.eng_sem
        assert len(self.stream_cnt) <= len(ctx.dma_sems), len(self.stream_cnt)
        stream_sem = {k: ctx.dma_sems[i] for i, k in enumerate(self.stream_cnt.keys())}
        for e in self.ENGS:
            n = 0
            for o in self.ops[e]:
                if o.marked and not o.is_dma:
                    n += 1
                    o.ordinal = n
        finals = list(self.stream_last.values())

        def token(d):
            if d.is_dma:
                return stream_sem[d.stream], 16 * d.sord
            return eng_sem[d.eng], d.ordinal

        def run(e, eng):
            waited = {}
            for o in self.ops[e]:
                need = {}
                for d in o.deps:
                    s, v = token(d)
                    key = id(s)
                    if v > waited.get(key, 0) and v > need.get(key, (None, 0))[1]:
                        need[key] = (s, v)
                for key, (s, v) in need.items():
                    eng.wait_ge(s, v)
                    waited[key] = v
                ins = o.fn(eng)
                if o.is_dma:
                    ins.then_inc(stream_sem[o.stream], 16)
                elif o.marked:
                    ins.then_inc(eng_sem[e], 1)
            if e == "sp":
                for d in finals:
                    s, v = token(d)
                    eng.wait_ge(s, v)

        with nc.Block() as block:
            @block.tensor
            def _(eng):
                run("pe", eng)

            @block.scalar
            def _(eng):
                run("act", eng)

            @block.vector
            def _(eng):
                run("dve", eng)

            @block.gpsimd
            def _(eng):
                run("pool", eng)

            @block.sync
            def _(eng):
                run("sp", eng)


def run_pipeline(tasks):
    n = len(tasks)
    if n == 0:
        return
    maxs = max(len(t) for t in tasks)
    for step in range(n + maxs):
        for s in reversed(range(maxs)):
            i = step - s
            if 0 <= i < n and s < len(tasks[i]):
                tasks[i][s]()


def build_program(S, NL, debug_out=None):
    NT = S // 512
    NK = S // 128
    HS = S // 2
    NTD = HS // 512
    NH = 2
    NCA = 2
    PAD = 64
    HALO = 2 * NL
    MB0 = 256
    MC0 = 384
    MOFF = PAD
    PW = 1280
    C_XA, C_YA, C_SQ, C_SK, C_SV, C_DQ, C_DK, C_DV = 0, 256, 512, 640, 768, 896, 1024, 1152
    GROUPS = [[0, 1], [2, 3], [4, 5], [6, 7]]
    nc = bass.Bass("TRN2", target_bir_lowering=False)

    def din(name, shape, dt=F32):
        return nc.dram_tensor(name, list(shape), dt, kind="ExternalInput").ap()

    def dscr(name, shape, dt):
        return nc.dram_tensor(name, list(shape), dt).ap()

    xT = din("xT", [D, S])
    xh = din("xh", [D, PAD + HS])
    norm1_g = din("norm1_g", [NL, 128, 8]); w_in = din("w_in", [NL, D, PW])
    conv_a_w = din("conv_a_w", [NL, 128, 4, NCA]); conv_a_b = din("conv_a_b", [NL, 128, NCA])
    w_rgate = din("w_rgate", [NL, 4, 64, 64]); b_rgate = din("b_rgate", [NL, 128, NCA])
    w_igate = din("w_igate", [NL, 4, 64, 64]); b_igate = din("b_igate", [NL, 128, NCA])
    lru_lambda = din("lru_lambda", [NL, 128, NCA])
    lam_q1 = din("lam_q1", [NL, 32]); lam_k1 = din("lam_k1", [NL, 32])
    lam_q2 = din("lam_q2", [NL, 32]); lam_k2 = din("lam_k2", [NL, 32])
    subln_g = din("subln_g", [NL, 64]); w_out = din("w_out", [NL, D, D])
    norm2_g = din("norm2_g", [NL, 128, 8]); w_ff_up = din("w_ff_up", [NL, D, 2 * D_FF])
    conv_ff_w = din("conv_ff_w", [NL, 128, 3, 44]); conv_ff_b = din("conv_ff_b", [NL, 128, 44])
    w_ff_down = din("w_ff_down", [NL, D_FF, D]); final_g = din("final_g", [128, 8])
    c_tri = din("c_tri", [128, 128], BF16)
    c_sbmask = din("c_sbmask", [128, 128])
    c_dfmask = din("c_dfmask", [NH, 128, 128])
    c_qaug = din("c_qaug", [NH, 4, S], BF16)
    c_kaug = din("c_kaug", [NH, 4, S], BF16)
    jmask = din("jmask", [128, 1])

    outT = nc.dram_tensor("outT", [D, HS], F32, kind="ExternalOutput").ap()
    dbg = None
    if debug_out is not None:
        dbg = nc.dram_tensor("dbg", list(debug_out[1]), debug_out[2], kind="ExternalOutput").ap()

    xbuf = dscr("xbuf", [D, PAD + HS], F32)
    hmain_t = [nc.dram_tensor("hmain%d" % i, [256, HS], BF16) for i in range(4)]
    hg_t = [nc.dram_tensor("hg%d" % i, [512, HS], BF16) for i in range(4)]
    xmidT = dscr("xmidT", [D, PAD + HS], F32)
    mixo_t = [nc.dram_tensor("mixo%d" % i, [64, PAD + S], BF16) for i in range(8)]
    mixg_t = [nc.dram_tensor("mixg%d" % i, [128, PAD + S], BF16) for i in range(8)]
    sqT = dscr("sqT", [128, S], BF16)
    skT = dscr("skT", [128, S], BF16)
    dqT = dscr("dqT", [NH, 2, 32, S], BF16)
    dkT = dscr("dkT", [NH, 2, 32, S], BF16)
    actT = dscr("actT", [D_FF, PAD + HS], BF16)

    top = ExitStack()
    ctx = Ctx(nc, top)
    cc_sem = top.enter_context(nc.semaphore("cc_sem"))
    cc_count = [0]
    _cnt = [0]

    def all_gather(src_ts, dst_ts):
        with nc.Block() as block:
            @block.gpsimd
            def _(g):
                for src_t, dst_t in zip(src_ts, dst_ts):
                    g.collective_compute("AllGather", ALU.bypass, replica_groups=GROUPS,
                                         ins=[src_t.ap().opt()], outs=[dst_t.ap().opt()]).then_inc(cc_sem)
                    cc_count[0] += 1
                g.wait_ge(cc_sem, cc_count[0])

    def uniq(name):
        _cnt[0] += 1
        return "%s_%d" % (name, _cnt[0])

    def chunked(ap2d):
        return ap2d.rearrange("(c p) t -> p c t", p=128)

    def vec_pc(ap1d):
        return ap1d.rearrange("(c p) -> p c", p=128)

    for l in range(NL):
        lam_init = 0.8 - 0.6 * float(np.exp(-0.3 * l))
        last = (l == NL - 1)

        with ExitStack() as st:
            def sb(name, shape, dt):
                return T(st.enter_context(nc.sbuf_tensor(uniq(name), list(shape), dt)), name)

            def ps(name, shape=(128, 512), dt=F32):
                return T(st.enter_context(nc.psum_tensor(uniq(name), list(shape), dt)), name)

            Vs = sb("Vs", [128, NK, NH * 64], BF16)
            Vd = sb("Vd", [128, NK, NH * 65 + 64], BF16)
            ones_bf = sb("ones_bf", [128, 128], BF16)
            tri = sb("tri", [128, 128], BF16)

            with ExitStack() as st2:
                def sb2(name, shape, dt):
                    return T(st2.enter_context(nc.sbuf_tensor(uniq(name), list(shape), dt)), name)

                def ps2(name, shape=(128, 512), dt=F32):
                    return T(st2.enter_context(nc.psum_tensor(uniq(name), list(shape), dt)), name)

                S_ = Sched(ctx)
                win = sb2("win", [128, 8, PW], BF16)
                wstage = [sb2("wstage%d" % i, [128, PW // 2], F32) for i in range(2)]
                g1 = sb2("g1", [128, 8], F32)
                caw = sb2("caw", [128, 4, NCA], F32)
                cab = sb2("cab", [128, NCA], F32)
                brg = sb2("brg", [128, NCA], F32)
                big = sb2("big", [128, NCA], F32)
                lam_t = sb2("lam_t", [128, NCA], F32)
                nsp = sb2("nsp", [128, NCA], F32)
                wr_st = sb2("wr_st", [128, NCA, 128], F32)
                wi_st = sb2("wi_st", [128, NCA, 128], F32)
                wr_bf = sb2("wr_bf", [128, NCA, 128], BF16)
                wi_bf = sb2("wi_bf", [128, NCA, 128], BF16)
                xt = [sb2("xt%d" % i, [128, 8, 512], F32) for i in range(1)]
                sqb = sb2("sqb", [128, 8, 512], BF16)
                rstd = sb2("rstd", [128, 512], F32)
                hT = sb2("hT", [128, 8, 512], BF16)
                xa_ext = [sb2("xa_ext%d" % c, [128, 3 + 512], F32) for c in range(NCA)]
                carry_a = [sb2("carry_a%d" % c, [128, 3], F32) for c in range(NCA)]
                state = [sb2("state%d" % c, [128, 1], F32) for c in range(NCA)]
                zpad = sb2("zpad", [128, PAD], BF16)
                qk_st = [sb2("qk_st%d" % i, [128, 512], BF16) for i in range(4)]
                pp = [ps2("pp%d" % i) for i in range(4)]
                pss = ps2("pss")
                pg = [ps2("pg%d" % i) for i in range(2)]

                if l > 0:
                    def gather_h(g):
                        ins = None
                        for src_t, dst_t in zip(hmain_t, hg_t):
                            ins = g.collective_compute("AllGather", ALU.bypass, replica_groups=GROUPS,
                                                       ins=[src_t.ap().opt()], outs=[dst_t.ap().opt()]).then_inc(cc_sem)
                            cc_count[0] += 1
                        return ins
                    S_.op("pool", gather_h)
                S_.op("pool", lambda e: e.memset(ones_bf[:], 1.0), writes=[ones_bf])
                S_.op("pool", lambda e: e.memset(zpad[:], 0.0), writes=[zpad])
                for r in range(8):
                    S_.op("sp", (lambda r=r: lambda e: e.dma_start(out=mixo_t[r].ap()[:, 0:PAD], in_=zpad[0:64, :]))(), reads=[zpad], writes=[("mixpad", r)], dma=True, stream=zpad)
                S_.op("sp", lambda e: e.dma_start(out=tri[:], in_=c_tri), writes=[tri], dma=True)
                S_.op("sp", lambda e: e.dma_start(out=g1[:], in_=norm1_g[l]), writes=[g1], dma=True)
                S_.op("sp", lambda e: e.dma_start(out=caw[:], in_=conv_a_w[l]), writes=[caw], dma=True)
                S_.op("sp", lambda e: e.dma_start(out=cab[:], in_=conv_a_b[l]), writes=[cab], dma=True)
                S_.op("sp", lambda e: e.dma_start(out=brg[:], in_=b_rgate[l]), writes=[brg], dma=True)
                S_.op("sp", lambda e: e.dma_start(out=big[:], in_=b_igate[l]), writes=[big], dma=True)
                S_.op("sp", lambda e: e.dma_start(out=lam_t[:], in_=lru_lambda[l]), writes=[lam_t], dma=True)
                S_.op("act", lambda e: e.activation(out=nsp[:], in_=lam_t[:], func=AF.Exp, scale=-1.0), reads=[lam_t], writes=[nsp])
                S_.op("act", lambda e: e.activation(out=nsp[:], in_=nsp[:], func=AF.Ln, bias=1.0, scale=1.0), reads=[nsp], writes=[nsp])
                S_.op("dve", lambda e: e.tensor_scalar(out=nsp[:], in0=nsp[:], scalar1=-8.0, scalar2=None, op0=ALU.mult), reads=[nsp], writes=[nsp])
                S_.op("pool", lambda e: e.memset(wr_st[:], 0.0), writes=[wr_st])
                S_.op("pool", lambda e: e.memset(wi_st[:], 0.0), writes=[wi_st])
                for c in range(NCA):
                    for hb in range(2):
                        n = 2 * c + hb
                        S_.op("sp", (lambda c=c, hb=hb, n=n: lambda e: e.dma_start(out=wr_st[hb * 64:(hb + 1) * 64, c, hb * 64:(hb + 1) * 64], in_=w_rgate[l, n]))(), reads=[], writes=[wr_st], dma=True, stream=("wr", 0))
                        S_.op("sp", (lambda c=c, hb=hb, n=n: lambda e: e.dma_start(out=wi_st[hb * 64:(hb + 1) * 64, c, hb * 64:(hb + 1) * 64], in_=w_igate[l, n]))(), reads=[], writes=[wi_st], dma=True, stream=("wi", 0))
                S_.op("dve", lambda e: e.tensor_copy(out=wr_bf[:], in_=wr_st[:]), reads=[wr_st], writes=[wr_bf])
                S_.op("dve", lambda e: e.tensor_copy(out=wi_bf[:], in_=wi_st[:]), reads=[wi_st], writes=[wi_bf])
                for kc in range(8):
                    for hf in range(2):
                        stg = wstage[hf]
                        S_.op("act" if hf else "sp", (lambda kc=kc, hf=hf, stg=stg: lambda e: e.dma_start(out=stg[:], in_=w_in[l, kc * 128:(kc + 1) * 128, hf * 640:(hf + 1) * 640]))(), writes=[stg], dma=True)
                        S_.op("act" if hf else "dve", (lambda kc=kc, hf=hf, stg=stg: lambda e: (e.copy if hf else e.tensor_copy)(out=win[:, kc, hf * 640:(hf + 1) * 640], in_=stg[:]))(), reads=[stg], writes=[(win, kc)])
                win_keys = [(win, kc) for kc in range(8)]
                for c in range(NCA):
                    S_.op("pool", (lambda c=c: lambda e: e.memset(carry_a[c][:], 0.0))(), writes=[carry_a[c]])
                    S_.op("pool", (lambda c=c: lambda e: e.memset(state[c][:], 0.0))(), writes=[state[c]])
                S_.op("pool", lambda e: e.memset(Vd[:], 1.0), writes=[("Vd", "all")])

                NSET = 2
                lset = [dict(ya=sb2("ya_%d" % i, [128, 512], F32), xc=sb2("xc_%d" % i, [128, 512], F32),
                             rg=sb2("rg_%d" % i, [128, 512], F32), ig=sb2("ig_%d" % i, [128, 512], F32),
                             av=sb2("av_%d" % i, [128, 512], F32), t1=sb2("t1_%d" % i, [128, 512], F32),
                             t2=sb2("t2_%d" % i, [128, 512], F32), xcb=sb2("xcb_%d" % i, [128, 512], BF16),
                             oa=sb2("oa_%d" % i, [128, 512], BF16)) for i in range(NSET)]
                xt2 = [xt[0], sb2("xt_b", [128, 8, 512], F32)]
                hT2 = [hT, sb2("hT_b", [128, 8, 512], BF16)]
                sqb2 = [sqb, sqb]
                rstd2 = [rstd, sb2("rstd_b", [128, 512], F32)]
                pss2 = [pss, ps2("pss_b")]
                npp = [0]
                nls = [0]

                def norm_task(ti):
                    t0 = ti * 512
                    xtt, hTt, sqt, rst, pst = xt2[ti % 2], hT2[ti % 2], sqb2[ti % 2], rstd2[ti % 2], pss2[ti % 2]

                    if l > 0:
                        def n0h():
                            hf_ = ti // NTD
                            tc0 = (ti % NTD) * 512
                            def ld_h(c, r0):
                                def fn(e):
                                    if ti == 0 and c == 0:
                                        e.wait_ge(cc_sem, cc_count[0])
                                    return e.dma_start(out=hTt[:, c, :], in_=hg_t[c // 2].ap()[r0:r0 + 128, tc0:tc0 + 512])
                                return fn
                            for c in range(8):
                                r0 = hf_ * 256 + (c % 2) * 128
                                S_.op("sp", ld_h(c, r0), writes=[(hTt, c)], dma=True, stream=(hTt, c))
                        return [n0h]

                    def n0():
                        xsrc = chunked(xT)[:, :, t0:t0 + 512]
                        S_.op("sp", lambda e: e.dma_start(out=xtt[:], in_=xsrc), writes=[xtt], dma=True)

                    def n1():
                        for c in range(8):
                            S_.op("pool", (lambda c=c: lambda e: e.tensor_tensor(out=sqt[:, c, :], in0=xtt[:, c, :], in1=xtt[:, c, :], op=ALU.mult))(), reads=[xtt] + [(xtt, "c", cc) for cc in range(8)], writes=[(sqt, c)])

                    def n2():
                        for c in range(8):
                            S_.op("pe", (lambda c=c: lambda e: e.matmul(pst[:], lhsT=ones_bf[:], rhs=sqt[:, c, :], start=(c == 0), stop=(c == 7)))(), reads=[ones_bf, (sqt, c)], writes=[pst])

                    def n3():
                        S_.op("act", lambda e: e.activation(out=rst[:], in_=pst[:], func=AF.Sqrt, bias=EPS, scale=1.0 / D), reads=[pst], writes=[rst])

                    def n4():
                        S_.op("dve", lambda e: e.reciprocal(out=rst[:], in_=rst[:]), reads=[rst], writes=[rst])

                    def n5():
                        for c in range(8):
                            S_.op("dve", (lambda c=c: lambda e: e.scalar_tensor_tensor(out=hTt[:, c, :], in0=xtt[:, c, :], scalar=g1[:, c:c + 1], in1=rst[:], op0=ALU.mult, op1=ALU.mult))(), reads=[xtt, g1, rst] + [(xtt, "c", cc) for cc in range(8)], writes=[(hTt, c)])
                    return [n0, n1, n2, n3, n4, n5]

                def proj_ops(pt, col0, hTt):
                    for kc in range(8):
                        S_.op("pe", (lambda kc=kc: lambda e: e.matmul(pt[:], lhsT=win[:, kc, col0:col0 + 128], rhs=hTt[:, kc, :], start=(kc == 0), stop=(kc == 7)))(), reads=[(win, kc), (hTt, kc)], writes=[pt])

                def qk_task(ti, nm, cbase, scl):
                    t0 = ti * 512
                    hTt = hT2[ti % 2]
                    pt = pp[npp[0] % 4]
                    stt = qk_st[npp[0] % 4]
                    npp[0] += 1

                    def q0():
                        proj_ops(pt, cbase, hTt)

                    def q1():
                        S_.op("act", lambda e: e.activation(out=stt[:], in_=pt[:], func=AF.Identity, scale=scl), reads=[pt], writes=[stt])

                    def q2():
                        if nm == "sq" or nm == "sk":
                            dst = sqT if nm == "sq" else skT
                            S_.op("sp", lambda e: e.dma_start(out=dst[0:128, t0:t0 + 512], in_=stt[:]), reads=[stt], writes=[(nm, ti)], dma=True, stream=stt)
                        else:
                            dst = dqT if nm == "dq" else dkT
                            for hh_ in range(2):
                                for m in range(2):
                                    r0 = hh_ * 64 + m * 32
                                    S_.op("sp", (lambda hh_=hh_, m=m, r0=r0: lambda e: e.dma_start(out=dst[hh_, m, :, t0:t0 + 512], in_=stt[r0:r0 + 32, :]))(), reads=[stt], writes=[(nm, ti, hh_, m)], dma=True, stream=stt)
                    return [q0, q1, q2]

                def v_task(ti, sub):
                    hTt = hT2[ti % 2]
                    pt = pp[npp[0] % 4]
                    npp[0] += 1
                    tt = ti * 4 + sub

                    def v0():
                        for half, cbase in enumerate((C_SV, C_DV)):
                            for kc in range(8):
                                S_.op("pe", (lambda kc=kc, half=half, cbase=cbase: lambda e: e.matmul(pt[:, half * 128:(half + 1) * 128], lhsT=hTt[:, kc, sub * 128:(sub + 1) * 128], rhs=win[:, kc, cbase:cbase + 128], start=(kc == 0), stop=(kc == 7)))(), reads=[(win, kc), (hTt, kc)], writes=[pt])

                    def v1():
                        S_.op("act", lambda e: e.copy(out=Vs[:, tt, :], in_=pt[:, 0:128]), reads=[pt], writes=[("Vs", ti, sub)])
                        S_.op("act", lambda e: e.copy(out=Vd[:, tt, 0:NH * 65].rearrange("p (h d) -> p h d", d=65)[:, :, 0:64], in_=pt[:, 128:256].rearrange("p (h d) -> p h d", d=64)), reads=[pt, ("Vd", "all")], writes=[("Vd", ti, sub)])
                    return [v0, v1]

                def lru_task(ti, c):
                    t0 = ti * 512
                    hTt = hT2[ti % 2]
                    pxa = pp[npp[0] % 4]
                    pya = pp[(npp[0] + 1) % 4]
                    npp[0] += 2
                    B_ = lset[nls[0] % NSET]
                    nls[0] += 1
                    ya, xc, rg, ig, av, t1, t2, xcb, oat = B_["ya"], B_["xc"], B_["rg"], B_["ig"], B_["av"], B_["t1"], B_["t2"], B_["xcb"], B_["oa"]
                    ext = xa_ext[c]

                    def a0():
                        proj_ops(pxa, C_XA + c * 128, hTt)
                        proj_ops(pya, C_YA + c * 128, hTt)

                    def a1():
                        S_.op("act", lambda e: e.copy(out=ext[:, 3:515], in_=pxa[:]), reads=[pxa], writes=[(ext, "m")])
                        S_.op("act", lambda e: e.copy(out=ya[:], in_=pya[:]), reads=[pya], writes=[ya])
                        S_.op("pool", lambda e: e.tensor_copy(out=ext[:, 0:3], in_=carry_a[c][:]), reads=[carry_a[c]], writes=[(ext, "h")])

                    def a2():
                        S_.op("dve", lambda e: e.tensor_scalar(out=xc[:], in0=ext[:, 3:515], scalar1=caw[:, 3, c:c + 1], scalar2=cab[:, c:c + 1], op0=ALU.mult, op1=ALU.add), reads=[(ext, "m"), caw, cab], writes=[xc])
                        S_.op("pool", lambda e: e.tensor_tensor(out=t1[:], in0=ya[:], in1=ya[:], op=ALU.mult), reads=[ya], writes=[t1])

                    def a3():
                        for k in range(3):
                            S_.op("dve", (lambda k=k: lambda e: e.scalar_tensor_tensor(out=xc[:], in0=ext[:, k:k + 512], scalar=caw[:, k, c:c + 1], in1=xc[:], op0=ALU.mult, op1=ALU.add))(), reads=[(ext, "m"), (ext, "h"), caw, xc], writes=[xc])
                        S_.op("pool", lambda e: e.tensor_scalar(out=t1[:], in0=t1[:], scalar1=0.044715, scalar2=1.0, op0=ALU.mult, op1=ALU.add), reads=[t1], writes=[t1])

                    def a4():
                        S_.op("pool", lambda e: e.tensor_copy(out=carry_a[c][:], in_=ext[:, 512:515]), reads=[(ext, "m")], writes=[carry_a[c]])
                        S_.op("pool", lambda e: e.tensor_copy(out=xcb[:], in_=xc[:]), reads=[xc], writes=[xcb])
                        S_.op("pool", lambda e: e.tensor_tensor(out=t1[:], in0=t1[:], in1=ya[:], op=ALU.mult), reads=[t1, ya], writes=[t1])

                    def a5():
                        S_.op("pe", lambda e: e.matmul(pg[0][:], lhsT=wr_bf[:, c, :], rhs=xcb[:], start=True, stop=True), reads=[wr_bf, xcb], writes=[pg[0]])
                        S_.op("pe", lambda e: e.matmul(pg[1][:], lhsT=wi_bf[:, c, :], rhs=xcb[:], start=True, stop=True), reads=[wi_bf, xcb], writes=[pg[1]])

                    def a6():
                        S_.op("act", lambda e: e.activation(out=rg[:], in_=pg[0][:], func=AF.Sigmoid, bias=brg[:, c:c + 1], scale=1.0), reads=[pg[0], brg], writes=[rg])
                        S_.op("act", lambda e: e.activation(out=ig[:], in_=pg[1][:], func=AF.Sigmoid, bias=big[:, c:c + 1], scale=1.0), reads=[pg[1], big], writes=[ig])
                        S_.op("act", lambda e: e.activation(out=t2[:], in_=t1[:], func=AF.Sigmoid, scale=1.5957691216057308), reads=[t1], writes=[t2])

                    def a7():
                        S_.op("act", lambda e: e.activation(out=av[:], in_=rg[:], func=AF.Exp, scale=nsp[:, c:c + 1]), reads=[rg, nsp], writes=[av])
                        S_.op("dve", lambda e: e.tensor_tensor(out=ig[:], in0=ig[:], in1=xc[:], op=ALU.mult), reads=[ig, xc], writes=[ig])
                        S_.op("pool", lambda e: e.tensor_tensor(out=t2[:], in0=t2[:], in1=ya[:], op=ALU.mult), reads=[t2, ya], writes=[t2])

                    def a8():
                        S_.op("pool", lambda e: e.tensor_tensor(out=rg[:], in0=av[:], in1=av[:], op=ALU.mult), reads=[av], writes=[rg])

                    def a9():
                        S_.op("act", lambda e: e.activation(out=rg[:], in_=rg[:], func=AF.Sqrt, bias=1.0, scale=-1.0), reads=[rg], writes=[rg])

                    def a10():
                        S_.op("dve", lambda e: e.tensor_tensor(out=ig[:], in0=ig[:], in1=rg[:], op=ALU.mult), reads=[ig, rg], writes=[ig])

                    def a11():
                        S_.op("dve", lambda e: e.tensor_tensor_scan(out=xc[:], data0=av[:], data1=ig[:], initial=state[c][:, 0:1], op0=ALU.mult, op1=ALU.add), reads=[av, ig, state[c]], writes=[xc])

                    def a12():
                        S_.op("pool", lambda e: e.tensor_copy(out=state[c][:], in_=xc[:, 511:512]), reads=[xc], writes=[state[c]])
                        S_.op("dve", lambda e: e.tensor_tensor(out=oat[:], in0=t2[:], in1=xc[:], op=ALU.mult), reads=[t2, xc], writes=[oat])

                    def a13():
                        for hq in range(2):
                            S_.op("sp", (lambda hq=hq: lambda e: e.dma_start(out=mixo_t[2 * c + hq].ap()[:, MOFF + t0:MOFF + t0 + 512], in_=oat[hq * 64:(hq + 1) * 64, :]))(), reads=[oat], writes=[("mixed_a", ti, c, hq)], dma=True, stream=oat)
                    return [a0, a1, a2, a3, a4, a5, a6, a7, a8, a9, a10, a11, a12, a13]

                qk_specs = [("sq", C_SQ, 0.125), ("sk", C_SK, 1.0), ("dq", C_DQ, 32 ** -0.5), ("dk", C_DK, 1.0)]
                tasks = [norm_task(0)]
                for ti in range(NT):
                    if ti + 1 < NT:
                        tasks.append(norm_task(ti + 1))
                    if ti == 0:
                        tasks += [[] for _ in range(6)]
                    for c in range(NCA):
                        tasks.append(lru_task(ti, c))
                    for (nm, cbase, scl) in qk_specs:
                        tasks.append(qk_task(ti, nm, cbase, scl))
                    for sub in range(4):
                        tasks.append(v_task(ti, sub))
                run_pipeline(tasks)
                S_.emit()

            with ExitStack() as st2:
                def sb2(name, shape, dt):
                    return T(st2.enter_context(nc.sbuf_tensor(uniq(name), list(shape), dt)), name)

                def ps2(name, shape=(128, 512), dt=F32):
                    return T(st2.enter_context(nc.psum_tensor(uniq(name), list(shape), dt)), name)

                S_ = Sched(ctx)
                kT = [sb2("kT%d" % i, [64, S], BF16) for i in range(2)]
                qT = [sb2("qT%d" % i, [64, S], BF16) for i in range(2)]
                sbm = sb2("sbm", [128, 128], F32)
                zeros_bf = sb2("zeros_bf", [128, 64], BF16)
                NE = 7
                E = [sb2("E%d" % i, [128, 2, 512], F32) for i in range(NE)]
                L = [sb2("L%d" % i, [128, 2, 512], BF16) for i in range(3)]
                tmp = [sb2("tmp%d" % i, [128, 2, 512], F32) for i in range(3)]
                eC = [sb2("eC%d" % i, [128, 2, 512], F32) for i in range(3)]
                Wt = [sb2("Wt%d" % i, [128, 2, 512], BF16) for i in range(3)]
                carry = sb2("carry", [128, 512], F32)
                osb = [sb2("osb%d" % i, [64, 512], BF16) for i in range(2)]
                pz = [ps2("pz%d" % i, (128, 2, 512)) for i in range(2)]
                pc = ps2("pc", (128, 2, 512))
                pb = ps2("pb")
                pot = ps2("po")
                S_.op("sp", lambda e: e.dma_start(out=sbm[:], in_=c_sbmask), writes=[sbm], dma=True)
                S_.op("pool", lambda e: e.memset(zeros_bf[:], 0.0), writes=[zeros_bf])
                tasks = []
                it = 0
                nblk = 0
                for h in range(NH):
                    kTh = kT[h % 2]
                    qTh = qT[h % 2]
                    S_.op("sp", (lambda kTh=kTh, h=h: lambda e: e.dma_start(out=kTh[:], in_=skT[h * 64:(h + 1) * 64, :]))(), writes=[kTh], dma=True)
                    S_.op("sp", (lambda qTh=qTh, h=h: lambda e: e.dma_start(out=qTh[:], in_=sqT[h * 64:(h + 1) * 64, :]))(), writes=[qTh], dma=True)
                    for bi in range(NT):
                        t0 = bi * 512
                        osbt = osb[nblk % 2]
                        nblk += 1
                        nkt = bi * 4 + 4
                        items = [(kt,) for kt in range(nkt - 1, bi * 4 - 1, -1)] + [(kt, kt - 1) for kt in range(bi * 4 - 1, 0, -2)]
                        for ii, kts in enumerate(items):
                            npr = len(kts)
                            diag = kts[0] - bi * 4 if npr == 1 else -1
                            c0 = 128 * diag if diag >= 0 else 0
                            first = (ii == 0)
                            lastk = (ii == len(items) - 1)
                            Eb, Lb, tb, eb, Wb = E[it % NE], L[it % 3], tmp[it % 3], eC[it % 3], Wt[it % 3]
                            pzb = pz[it % 2]
                            it += 1

                            def s0(pzb=pzb, kTh=kTh, qTh=qTh, kts=kts, c0=c0, t0=t0, S_=S_):
                                for j, kt in enumerate(kts):
                                    S_.op("pe", (lambda j=j, kt=kt: lambda e: e.matmul(pzb[:, j, c0:512], lhsT=kTh[:, kt * 128:(kt + 1) * 128], rhs=qTh[:, t0 + c0:t0 + 512], start=True, stop=True))(), reads=[kTh, qTh], writes=[pzb])

                            def s1(Eb=Eb, pzb=pzb, c0=c0, npr=npr, S_=S_):
                                S_.op("act", lambda e: e.activation(out=Eb[:, 0:npr, c0:512], in_=pzb[:, 0:npr, c0:512], func=AF.Exp), reads=[pzb], writes=[Eb])

                            def s2(Eb=Eb, Lb=Lb, c0=c0, npr=npr, S_=S_):
                                S_.op("act", lambda e: e.activation(out=Lb[:, 0:npr, c0:512], in_=Eb[:, 0:npr, c0:512], func=AF.Ln, bias=1.0, scale=1.0), reads=[Eb], writes=[Lb])

                            def s3(Eb=Eb, Lb=Lb, c0=c0, diag=diag, npr=npr, S_=S_):
                                if diag >= 0:
                                    S_.op("dve", lambda e: e.tensor_tensor(out=Eb[:, 0, c0:c0 + 128], in0=Eb[:, 0, c0:c0 + 128], in1=sbm[:], op=ALU.mult), reads=[Eb, sbm], writes=[Eb])
                                    S_.op("dve", lambda e: e.tensor_tensor(out=Lb[:, 0, c0:c0 + 128], in0=Lb[:, 0, c0:c0 + 128], in1=sbm[:], op=ALU.mult), reads=[Lb, sbm], writes=[Lb])
                                S_.op("pe", lambda e: e.matmul(pc[:, 0, c0:512], lhsT=tri[:], rhs=Lb[:, 0, c0:512], start=True, stop=True), reads=[tri, Lb], writes=[pc])
                                if npr == 2:
                                    S_.op("pe", lambda e: e.matmul(pc[:, 1, :], lhsT=tri[:], rhs=Lb[:, 1, :], start=True, stop=False), reads=[tri, Lb], writes=[pc])
                                    S_.op("pe", lambda e: e.matmul(pc[:, 1, :], lhsT=ones_bf[:], rhs=Lb[:, 0, :], start=False, stop=True), reads=[ones_bf, Lb], writes=[pc])
                                    S_.op("pe", lambda e: e.matmul(pb[:], lhsT=ones_bf[:], rhs=Lb[:, 0, :], start=True, stop=False), reads=[ones_bf, Lb], writes=[pb])
                                    S_.op("pe", lambda e: e.matmul(pb[:], lhsT=ones_bf[:], rhs=Lb[:, 1, :], start=False, stop=True), reads=[ones_bf, Lb], writes=[pb])
                                else:
                                    S_.op("pe", lambda e: e.matmul(pb[:, c0:512], lhsT=ones_bf[:], rhs=Lb[:, 0, c0:512], start=True, stop=True), reads=[ones_bf, Lb], writes=[pb])

                            def s4(tb=tb, c0=c0, first=first, npr=npr, S_=S_):
                                if first:
                                    S_.op("pool", lambda e: e.memset(carry[:], 0.0), writes=[carry])
                                for j in range(npr):
                                    S_.op("dve", (lambda j=j: lambda e: e.tensor_tensor(out=tb[:, j, c0:512], in0=pc[:, j, c0:512], in1=carry[:, c0:512], op=ALU.add))(), reads=[pc, carry], writes=[tb])
                                S_.op("dve", lambda e: e.tensor_tensor(out=carry[:, c0:512], in0=pb[:, c0:512], in1=carry[:, c0:512], op=ALU.add), reads=[pb, carry], writes=[carry])

                            def s5(eb=eb, tb=tb, c0=c0, npr=npr, S_=S_):
                                S_.op("act", lambda e: e.activation(out=eb[:, 0:npr, c0:512], in_=tb[:, 0:npr, c0:512], func=AF.Exp, scale=-1.0), reads=[tb], writes=[eb])

                            def s6(Wb=Wb, Eb=Eb, eb=eb, c0=c0, npr=npr, S_=S_):
                                S_.op("pool", lambda e: e.tensor_tensor(out=Wb[:, 0:npr, c0:512], in0=Eb[:, 0:npr, c0:512], in1=eb[:, 0:npr, c0:512], op=ALU.mult), reads=[Eb, eb], writes=[Wb])

                            def s7(Wb=Wb, kts=kts, h=h, c0=c0, first=first, lastk=lastk, qTh=qTh, t0=t0, S_=S_):
                                if first:
                                    S_.op("pe", lambda e: e.matmul(pot[0:64, :], lhsT=zeros_bf[0:64, :], rhs=qTh[:, t0:t0 + 512], start=True, stop=False), reads=[zeros_bf, qTh], writes=[pot])
                                for j, kt in enumerate(kts):
                                    S_.op("pe", (lambda j=j, kt=kt: lambda e: e.matmul(pot[0:64, c0:512], lhsT=Vs[:, kt, h * 64:(h + 1) * 64], rhs=Wb[:, j, c0:512], start=False, stop=(lastk and j == len(kts) - 1)))(), reads=[Wb], writes=[pot])

                            stages = [s0, s1, s2, s3, s4, s5, s6, s7]
                            if lastk:
                                def s8(osbt=osbt, S_=S_):
                                    S_.op("act", lambda e: e.copy(out=osbt[:], in_=pot[0:64, :]), reads=[pot], writes=[osbt])

                                def s9(osbt=osbt, h=h, t0=t0, S_=S_):
                                    S_.op("sp", lambda e: e.dma_start(out=mixo_t[4 + h].ap()[:, MOFF + t0:MOFF + t0 + 512], in_=osbt[:]), reads=[osbt], writes=[("mixed_b", h, t0)], dma=True, stream=osbt)
                                stages += [s8, s9]
                            tasks.append(stages)
                run_pipeline(tasks)
                S_.emit()

            with ExitStack() as st2:
                def sb2(name, shape, dt):
                    return T(st2.enter_context(nc.sbuf_tensor(uniq(name), list(shape), dt)), name)

                def ps2(name, shape=(128, 512), dt=F32):
                    return T(st2.enter_context(nc.psum_tensor(uniq(name), list(shape), dt)), name)

                S_ = Sched(ctx)
                Ka = [sb2("Ka%d" % i, [128, S], BF16) for i in range(2)]
                Qa = [sb2("Qa%d" % i, [128, S], BF16) for i in range(2)]
                dfm = sb2("dfm", [128, NH, 128], F32)
                NP = 5
                P = [sb2("P%d" % i, [128, 2, 512], BF16) for i in range(NP)]
                lamv = sb2("lamv", [128, 4, 32], F32)
                lamp = sb2("lamp", [128, 2, 32], F32)
                lams = sb2("lams", [128, 4], F32)
                ones_f = sb2("ones_f", [128, 64], BF16)
                gs = sb2("gs", [64, 1], F32)
                pvs = [[sb2("pvs%d_%d" % (i, m), [128, 512], F32) for m in range(2)] for i in range(2)]
                rden = [sb2("rden%d" % i, [128, 2, 512], F32) for i in range(2)]
                rh = [sb2("rh%d" % i, [128, 2, 512], BF16) for i in range(2)]
                rl = [sb2("rl%d" % i, [128, 2, 512], BF16) for i in range(2)]
                rb = [[sb2("rb%d_%d" % (i, m), [64, 512], F32) for m in range(2)] for i in range(2)]
                od = [sb2("od%d" % i, [64, 512], F32) for i in range(2)]
                od2 = [sb2("od2%d" % i, [64, 512], F32) for i in range(2)]
                osq = [sb2("osq%d" % i, [64, 512], BF16) for i in range(2)]
                rs = [sb2("rs%d" % i, [64, 512], F32) for i in range(2)]
                oc = [sb2("oc%d" % i, [64, 512], BF16) for i in range(2)]
                psc = [ps2("psc%d" % i, (128, 2, 512)) for i in range(2)]
                pov = [ps2("pov%d" % m) for m in range(2)]
                pbc = [ps2("pbc%d" % i) for i in range(2)]

                S_.op("sp", lambda e: e.dma_start(out=dfm[:], in_=c_dfmask.rearrange("h k q -> k h q")), writes=[dfm], dma=True)
                S_.op("pool", lambda e: e.memset(ones_f[:], 1.0), writes=[ones_f])
                for i in range(2):
                    for m in range(2):
                        S_.op("pool", (lambda i=i: lambda e: e.memset(Ka[i][:], 0.0))() if m == 0 else (lambda i=i: lambda e: e.memset(Qa[i][:], 0.0))(), writes=[(Ka[i], mm, kk) for mm in range(2) for kk in range(2)] if m == 0 else [(Qa[i], mm, kk) for mm in range(2) for kk in range(2)])
                S_.op("sp", lambda e: e.dma_start(out=gs[:], in_=subln_g[l].rearrange("(d o) -> d o", o=1)), writes=[gs], dma=True)
                S_.op("dve", lambda e: e.tensor_scalar(out=gs[:], in0=gs[:], scalar1=float(1.0 - lam_init), scalar2=None, op0=ALU.mult), reads=[gs], writes=[gs])
                lamkeys = [(lamv, i) for i in range(4)]
                for i, v in enumerate((lam_q1, lam_k1, lam_q2, lam_k2)):
                    S_.op("sp", (lambda i=i, v=v: lambda e: e.dma_start(out=lamv[64:65, i, :], in_=v[l:l + 1, :]))(), writes=[(lamv, i)], dma=True, stream=("lamv", 0))
                S_.op("dve", lambda e: e.tensor_tensor(out=lamp[64:65, 0, :], in0=lamv[64:65, 0, :], in1=lamv[64:65, 1, :], op=ALU.mult), reads=lamkeys, writes=[(lamp, 0)])
                S_.op("dve", lambda e: e.tensor_tensor(out=lamp[64:65, 1, :], in0=lamv[64:65, 2, :], in1=lamv[64:65, 3, :], op=ALU.mult), reads=lamkeys, writes=[(lamp, 1)])
                S_.op("dve", lambda e: e.reduce_sum(out=lams[64:65, 0:1], in_=lamp[64:65, 0, :], axis=mybir.AxisListType.X), reads=[(lamp, 0)], writes=[(lams, 0)])
                S_.op("dve", lambda e: e.reduce_sum(out=lams[64:65, 1:2], in_=lamp[64:65, 1, :], axis=mybir.AxisListType.X), reads=[(lamp, 1)], writes=[(lams, 1)])
                S_.op("act", lambda e: e.activation(out=lams[64:65, 0:2], in_=lams[64:65, 0:2], func=AF.Exp), reads=[(lams, 0), (lams, 1)], writes=[(lams, 2)])
                S_.op("dve", lambda e: e.tensor_tensor(out=lams[64:65, 2:3], in0=lams[64:65, 1:2], in1=lams[64:65, 0:1], op=ALU.subtract), reads=[(lams, 2)], writes=[(lams, 3)])
                S_.op("dve", lambda e: e.tensor_scalar(out=lams[64:65, 3:4], in0=lams[64:65, 2:3], scalar1=float(-lam_init), scalar2=None, op0=ALU.add), reads=[(lams, 3)], writes=[(lams, 4)])
                neglam = lams[64:65, 3:4]

                def load_dhead(Kah, Qah, h, S_=S_):
                    for m in range(2):
                        S_.op("sp", (lambda m=m: lambda e: e.dma_start(out=Kah[m * 64:m * 64 + 32, :], in_=dkT[h, m]))(), writes=[(Kah, m, 0)], dma=True, stream=(Kah, 0))
                        S_.op("sp", (lambda m=m: lambda e: e.dma_start(out=Kah[m * 64 + 32:m * 64 + 36, :], in_=c_kaug[h]))(), writes=[(Kah, m, 1)], dma=True, stream=(Kah, 0))
                        S_.op("sp", (lambda m=m: lambda e: e.dma_start(out=Qah[m * 64:m * 64 + 32, :], in_=dqT[h, m]))(), writes=[(Qah, m, 0)], dma=True, stream=(Qah, 0))
                        S_.op("sp", (lambda m=m: lambda e: e.dma_start(out=Qah[m * 64 + 32:m * 64 + 36, :], in_=c_qaug[h]))(), writes=[(Qah, m, 1)], dma=True, stream=(Qah, 0))

                tasks = []
                it = 0
                nblk = 0
                for h in range(NH):
                    Kah = Ka[h % 2]
                    Qah = Qa[h % 2]
                    if h < 2:
                        load_dhead(Kah, Qah, h)
                    kqkeys = [(Kah, 0, 0), (Kah, 0, 1), (Kah, 1, 0), (Kah, 1, 1), (Qah, 0, 0), (Qah, 0, 1), (Qah, 1, 0), (Qah, 1, 1)]
                    for bi in range(NT):
                        t0 = bi * 512
                        par = nblk % 2
                        nblk += 1
                        nkt = bi * 4 + 4
                        for kt in range(nkt):
                            diag = kt - bi * 4
                            c0 = 128 * diag if diag >= 0 else 0
                            Pb = P[it % NP]
                            pscb = psc[it % 2]
                            it += 1

                            def s0(pscb=pscb, Kah=Kah, Qah=Qah, kt=kt, c0=c0, t0=t0, kqkeys=kqkeys, S_=S_):
                                for m in range(2):
                                    p0 = m * 64
                                    S_.op("pe", (lambda m=m, p0=p0: lambda e: e.matmul(pscb[:, m, c0:512], lhsT=Kah[p0:p0 + 64, kt * 128:(kt + 1) * 128], rhs=Qah[p0:p0 + 64, t0 + c0:t0 + 512], start=True, stop=True))(), reads=kqkeys, writes=[(pscb, m)])

                            def s1(Pb=Pb, pscb=pscb, c0=c0, S_=S_):
                                S_.op("act", lambda e: e.activation(out=Pb[:, :, c0:512], in_=pscb[:, :, c0:512], func=AF.Exp), reads=[(pscb, 0), (pscb, 1)], writes=[(Pb, 0), (Pb, 1)])

                            def s2(Pb=Pb, c0=c0, h=h, diag=diag, S_=S_):
                                if diag >= 0:
                                    for m in range(2):
                                        S_.op("dve", (lambda m=m: lambda e: e.tensor_tensor(out=Pb[:, m, c0:c0 + 128], in0=Pb[:, m, c0:c0 + 128], in1=dfm[:, h, :], op=ALU.mult))(), reads=[(Pb, m), dfm], writes=[(Pb, m)])

                            def s3(Pb=Pb, kt=kt, h=h, c0=c0, nkt=nkt, S_=S_):
                                for m in range(2):
                                    S_.op("pe", (lambda m=m: lambda e: e.matmul(pov[m][:, c0:512], lhsT=Vd[:, kt, h * 65:h * 65 + 128], rhs=Pb[:, m, c0:512], start=(kt == 0), stop=(kt == nkt - 1)))(), reads=[(Pb, m)], writes=[pov[m]])

                            stages = [s0, s1, s2, s3]
                            if kt == nkt - 1:
                                pv0, pv1 = pvs[par][0], pvs[par][1]
                                rd, rh_, rl_, rb_, od_, od2_, osq_, rs_, oc_ = rden[par], rh[par], rl[par], rb[par], od[par], od2[par], osq[par], rs[par], oc[par]

                                def ec(pv0=pv0, pv1=pv1, S_=S_):
                                    S_.op("act", lambda e: e.copy(out=pv0[0:65, :], in_=pov[0][0:65, :]), reads=[pov[0]], writes=[pv0])
                                    S_.op("act", lambda e: e.copy(out=pv1[0:65, :], in_=pov[1][0:65, :]), reads=[pov[1]], writes=[pv1])

                                def e0(pv0=pv0, pv1=pv1, rd=rd, S_=S_):
                                    S_.op("dve", lambda e: e.reciprocal(out=rd[64:65, 0, :], in_=pv0[64:65, :]), reads=[pv0], writes=[(rd, 0)])
                                    S_.op("dve", lambda e: e.reciprocal(out=rd[64:65, 1, :], in_=pv1[64:65, :]), reads=[pv1], writes=[(rd, 1)])

                                def e1(rd=rd, S_=S_):
                                    S_.op("dve", lambda e: e.tensor_scalar(out=rd[64:65, 1, :], in0=rd[64:65, 1, :], scalar1=neglam, scalar2=None, op0=ALU.mult), reads=[(rd, 1), (lams, 4)], writes=[(rd, 1)])

                                def e2(rd=rd, rh_=rh_, S_=S_):
                                    S_.op("dve", lambda e: e.tensor_copy(out=rh_[64:65, :, :], in_=rd[64:65, :, :]), reads=[(rd, 0), (rd, 1)], writes=[rh_])

                                def e3(rd=rd, rh_=rh_, rl_=rl_, S_=S_):
                                    S_.op("dve", lambda e: e.tensor_tensor(out=rl_[64:65, :, :], in0=rd[64:65, :, :], in1=rh_[64:65, :, :], op=ALU.subtract), reads=[(rd, 0), (rd, 1), rh_], writes=[rl_])

                                def e4(rh_=rh_, rl_=rl_, S_=S_):
                                    for mm in range(2):
                                        S_.op("pe", (lambda mm=mm: lambda e: e.matmul(pbc[mm][0:64, :], lhsT=ones_f[64:65, :], rhs=rh_[64:65, mm, :], start=True, stop=False))(), reads=[ones_f, rh_], writes=[pbc[mm]])
                                        S_.op("pe", (lambda mm=mm: lambda e: e.matmul(pbc[mm][0:64, :], lhsT=ones_f[64:65, :], rhs=rl_[64:65, mm, :], start=False, stop=True))(), reads=[ones_f, rl_], writes=[pbc[mm]])

                                def e5(rb_=rb_, S_=S_):
                                    for mm in range(2):
                                        S_.op("act", (lambda mm=mm: lambda e: e.copy(out=rb_[mm][:], in_=pbc[mm][0:64, :]))(), reads=[pbc[mm]], writes=[rb_[mm]])

                                def e6(pv0=pv0, pv1=pv1, rb_=rb_, od_=od_, od2_=od2_, S_=S_):
                                    S_.op("dve", lambda e: e.tensor_tensor(out=od_[:], in0=pv0[0:64, :], in1=rb_[0][:], op=ALU.mult), reads=[pv0, rb_[0]], writes=[od_])
                                    S_.op("dve", lambda e: e.tensor_tensor(out=od2_[:], in0=pv1[0:64, :], in1=rb_[1][:], op=ALU.mult), reads=[pv1, rb_[1]], writes=[od2_])

                                def e7(od_=od_, od2_=od2_, S_=S_):
                                    S_.op("dve", lambda e: e.tensor_tensor(out=od_[:], in0=od_[:], in1=od2_[:], op=ALU.add), reads=[od_, od2_], writes=[od_])

                                def e8(od_=od_, osq_=osq_, S_=S_):
                                    S_.op("pool", lambda e: e.tensor_tensor(out=osq_[:], in0=od_[:], in1=od_[:], op=ALU.mult), reads=[od_], writes=[osq_])

                                def e9(osq_=osq_, S_=S_):
                                    S_.op("pe", lambda e: e.matmul(pbc[0][0:64, :], lhsT=ones_bf[0:64, 0:64], rhs=osq_[:], start=True, stop=True), reads=[ones_bf, osq_], writes=[pbc[0]])

                                def e10(rs_=rs_, S_=S_):
                                    S_.op("act", lambda e: e.activation(out=rs_[:], in_=pbc[0][0:64, :], func=AF.Sqrt, bias=EPS, scale=1.0 / 64), reads=[pbc[0]], writes=[rs_])

                                def e11(rs_=rs_, S_=S_):
                                    S_.op("dve", lambda e: e.reciprocal(out=rs_[:], in_=rs_[:]), reads=[rs_], writes=[rs_])

                                def e12(od_=od_, rs_=rs_, oc_=oc_, S_=S_):
                                    S_.op("dve", lambda e: e.scalar_tensor_tensor(out=oc_[:], in0=od_[:], scalar=gs[:, 0:1], in1=rs_[:], op0=ALU.mult, op1=ALU.mult), reads=[od_, gs, rs_], writes=[oc_])

                                def e13(oc_=oc_, h=h, t0=t0, S_=S_):
                                    S_.op("sp", lambda e: e.dma_start(out=mixo_t[6 + h].ap()[:, MOFF + t0:MOFF + t0 + 512], in_=oc_[:]), reads=[oc_], writes=[("mixed_c", h, t0)], dma=True, stream=oc_)
                                def e45(e4=e4, e5=e5):
                                    e4(); e5()

                                def e910(e9=e9, e10=e10):
                                    e9(); e10()
                                stages += [ec, e0, e1, e2, e3, e45, e6, e7, e8, e910, e11, e12, e13]
                            tasks.append(stages)
                    if h + 2 < NH:
                        tasks.append([(lambda Kah=Kah, Qah=Qah, hn=h + 2: lambda: load_dhead(Kah, Qah, hn))()])
                run_pipeline(tasks)
                S_.emit()


        dtiles = [(-HALO, HALO)] + [(k * 512, 512) for k in range(NTD)]
        with ExitStack() as st2:
            def sb2(name, shape, dt):
                return T(st2.enter_context(nc.sbuf_tensor(uniq(name), list(shape), dt)), name)

            def ps2(name, shape=(128, 512), dt=F32):
                return T(st2.enter_context(nc.psum_tensor(uniq(name), list(shape), dt)), name)

            S_ = Sched(ctx)
            wo = sb2("wo", [128, 8, D], BF16)
            wup = sb2("wup", [128, 8, 2 * D_FF], BF16)
            wst = [sb2("wst%d" % i, [128, D_FF // 2], F32) for i in range(2)]
            g2 = sb2("g2", [128, 8], F32)
            cfw = sb2("cfw", [128, 3, 44], F32)
            cfb = sb2("cfb", [128, 44], F32)
            ones_bf = sb2("ones_bf2", [128, 128], BF16)
            xtc = [sb2("xtc%d" % i, [128, 512], F32) for i in range(2)]
            mx = [sb2("mx%d" % i, [128, 8, 512], BF16) for i in range(1)]
            sqbD = sb2("sqbD", [128, 8, 512], BF16)
            mxB = sb2("mxB", [128, 8, 512], BF16)
            jm = sb2("jmD", [128, 1], F32)
            jm1 = sb2("jm1D", [128, 1], F32)
            xm = sb2("xm", [128, 8, 512], F32)
            rstd = sb2("rstd", [128, 512], F32)
            h2 = sb2("h2", [128, 8, 512], BF16)
            sqb = h2
            uext = [sb2("uext%d" % i, [128, 2 + 512], F32) for i in range(4)]
            ucar = sb2("ucar", [128, 44, 2], F32)
            acc = [sb2("acc%d" % i, [128, 512], F32) for i in range(6)]
            sg = [sb2("sg%d" % i, [128, 512], F32) for i in range(2)]
            act_sb = [sb2("act_sb%d" % i, [128, 512], BF16) for i in range(3)]
            ppP = [ps2("ppP%d" % i) for i in range(2)]
            ppF = [ps2("ppF%d" % i) for i in range(4)]
            pss = ps2("pss")

            def gather_mixed(g):
                ins = None
                for src_t, dst_t in zip(mixo_t, mixg_t):
                    ins = g.collective_compute("AllGather", ALU.bypass, replica_groups=GROUPS,
                                               ins=[src_t.ap().opt()], outs=[dst_t.ap().opt()]).then_inc(cc_sem)
                    cc_count[0] += 1
                return ins
            S_.op("pool", gather_mixed)
            S_.op("pool", lambda e: e.memset(ones_bf[:], 1.0), writes=[ones_bf])
            S_.op("pool", lambda e: e.memset(ucar[:], 0.0), writes=[ucar])
            S_.op("sp", lambda e: e.dma_start(out=g2[:], in_=norm2_g[l]), writes=[g2], dma=True)
            S_.op("sp", lambda e: e.dma_start(out=jm[:], in_=jmask), writes=[jm], dma=True)
            S_.op("dve", lambda e: e.tensor_scalar(out=jm1[:], in0=jm[:], scalar1=-1.0, scalar2=1.0, op0=ALU.mult, op1=ALU.add), reads=[jm], writes=[jm1])
            S_.op("sp", lambda e: e.dma_start(out=cfw[:], in_=conv_ff_w[l]), writes=[cfw], dma=True)
            S_.op("sp", lambda e: e.dma_start(out=cfb[:], in_=conv_ff_b[l]), writes=[cfb], dma=True)
            nst = 0
            for kc in range(8):
                stg = wst[nst % 2]; nst += 1
                S_.op("act" if kc % 2 else "sp", (lambda kc=kc, stg=stg: lambda e: e.dma_start(out=stg[:, 0:D], in_=w_out[l, kc * 128:(kc + 1) * 128, :]))(), writes=[stg], dma=True)
                S_.op("act" if kc % 2 else "dve", (lambda kc=kc, stg=stg: lambda e: (e.copy if kc % 2 else e.tensor_copy)(out=wo[:, kc, :], in_=stg[:, 0:D]))(), reads=[stg], writes=[(wo, kc)])
            HW = D_FF // 2
            for kc in range(8):
                for hf in range(4):
                    stg = wst[nst % 2]; nst += 1
                    S_.op("act" if hf % 2 else "sp", (lambda kc=kc, hf=hf, stg=stg: lambda e: e.dma_start(out=stg[:], in_=w_ff_up[l, kc * 128:(kc + 1) * 128, hf * HW:(hf + 1) * HW]))(), writes=[stg], dma=True)
                    S_.op("act" if hf % 2 else "dve", (lambda kc=kc, hf=hf, stg=stg: lambda e: (e.copy if hf % 2 else e.tensor_copy)(out=wup[:, kc, hf * HW:(hf + 1) * HW], in_=stg[:]))(), reads=[stg], writes=[(wup, kc)])

            xbuf_c = chunked(xbuf)
            xh_c = chunked(xh)
            xmid_c = chunked(xmidT)
            act_c = chunked(actT)
            mxt = mx[0]
            nF = [0]

            def prologue_task(tix, rel, w):
                ldk = [(mxt, "ld", kc) for kc in range(8)] + [(mxB, "ld", kc) for kc in range(8)]
                xs_c = xh_c if l == 0 else xbuf_c

                def p0():
                    def ld_a(kc):
                        def fn(e):
                            if tix == 0 and kc == 0:
                                e.wait_ge(cc_sem, cc_count[0])
                            return e.dma_start(out=mxt[:, kc, 0:w], in_=mixg_t[kc].ap()[:, PAD + rel:PAD + rel + w])
                        return fn
                    for kc in range(8):
                        S_.op("sp", ld_a(kc), writes=[(mxt, "ld", kc), mxt] if kc == 0 else [(mxt, "ld", kc)], dma=True, stream=(mxt, "ld"))
                        S_.op("sp", (lambda kc=kc: lambda e: e.dma_start(out=mxB[:, kc, 0:w], in_=mixg_t[kc].ap()[:, PAD + HS + rel:PAD + HS + rel + w]))(), writes=[(mxB, "ld", kc)], dma=True, stream=(mxB, "ld"))

                def p1():
                    for kc in range(8):
                        S_.op("act", (lambda kc=kc: lambda e: e.activation(out=mxt[:, kc, 0:w], in_=mxt[:, kc, 0:w], func=AF.Identity, scale=jm1[:, 0:1]))(), reads=ldk + [jm1], writes=[(mxt, "a", kc)])

                def p2():
                    for kc in range(8):
                        S_.op("dve", (lambda kc=kc: lambda e: e.scalar_tensor_tensor(out=mxt[:, kc, 0:w], in0=mxB[:, kc, 0:w], scalar=jm[:, 0:1], in1=mxt[:, kc, 0:w], op0=ALU.mult, op1=ALU.add))(), reads=ldk + [(mxt, "a", kc), jm], writes=[(mxt, "s", kc)] + ([mxt] if kc == 7 else []))

                def wout(cos):
                    for co in cos:
                        pt = ppP[co % 2]
                        xt_ = xtc[co % 2]
                        S_.op("sp", (lambda co=co, xt_=xt_: lambda e: e.dma_start(out=xt_[:, 0:w], in_=xs_c[:, co, PAD + rel:PAD + rel + w]))(), writes=[xt_], dma=True)
                        for kc in range(8):
                            S_.op("pe", (lambda kc=kc, co=co, pt=pt: lambda e: e.matmul(pt[:, 0:w], lhsT=wo[:, kc, co * 128:(co + 1) * 128], rhs=mxt[:, kc, 0:w], start=(kc == 0), stop=(kc == 7)))(), reads=[(wo, kc), mxt], writes=[pt])
                        S_.op("dve", (lambda co=co, pt=pt, xt_=xt_: lambda e: e.tensor_tensor(out=xm[:, co, 0:w], in0=pt[:, 0:w], in1=xt_[:, 0:w], op=ALU.add))(), reads=[pt, xt_], writes=[(xm, co)])
                        S_.op("pool", (lambda co=co: lambda e: e.tensor_tensor(out=sqbD[:, co, 0:w], in0=xm[:, co, 0:w], in1=xm[:, co, 0:w], op=ALU.mult))(), reads=[(xm, co)], writes=[(sqbD, co)])

                def p3():
                    wout(range(0, 4))

                def p4():
                    wout(range(4, 8))

                def p5():
                    S_.op("sp", lambda e: e.dma_start(out=xmid_c[:, :, PAD + rel:PAD + rel + w], in_=xm[:, :, 0:w]), reads=[(xm, co) for co in range(8)], writes=[("xmid", tix)], dma=True, stream=xm)
                    for c in range(8):
                        S_.op("pe", (lambda c=c: lambda e: e.matmul(pss[:, 0:w], lhsT=ones_bf[:], rhs=sqbD[:, c, 0:w], start=(c == 0), stop=(c == 7)))(), reads=[ones_bf, (sqbD, c)], writes=[pss])

                def p6():
                    S_.op("act", lambda e: e.activation(out=rstd[:, 0:w], in_=pss[:, 0:w], func=AF.Sqrt, bias=EPS, scale=1.0 / D), reads=[pss], writes=[rstd])

                def p7():
                    S_.op("dve", lambda e: e.reciprocal(out=rstd[:, 0:w], in_=rstd[:, 0:w]), reads=[rstd], writes=[rstd])
                return [p0, p1, p2, p3, p4, p5, p6, p7]

            def h_task(tix, rel, w):
                def h0():
                    for c in range(8):
                        S_.op("dve", (lambda c=c: lambda e: e.scalar_tensor_tensor(out=h2[:, c, 0:w], in0=xm[:, c, 0:w], scalar=g2[:, c:c + 1], in1=rstd[:, 0:w], op0=ALU.mult, op1=ALU.mult))(), reads=[(xm, c), g2, rstd], writes=[(h2, c)])
                return [h0]

            def ffn_task(tix, rel, w, fc):
                k = nF[0]
                nF[0] += 1
                pts = [ppF[(2 * k) % 4], ppF[(2 * k + 1) % 4]]
                ues = [uext[(2 * k) % 4], uext[(2 * k + 1) % 4]]
                acs = [acc[(2 * k) % 6], acc[(2 * k + 1) % 6]]
                ccs = [fc, 22 + fc]
                sgt = sg[k % 2]
                asb = act_sb[k % 3]

                def f0():
                    for gv in range(2):
                        col0 = ccs[gv] * 128
                        for kc in range(8):
                            S_.op("pe", (lambda kc=kc, col0=col0, pt=pts[gv]: lambda e: e.matmul(pt[:, 0:w], lhsT=wup[:, kc, col0:col0 + 128], rhs=h2[:, kc, 0:w], start=(kc == 0), stop=(kc == 7)))(), reads=[(wup, kc), (h2, kc)], writes=[pts[gv]])

                def f1():
                    for gv in range(2):
                        ue, pt, ac, cc = ues[gv], pts[gv], acs[gv], ccs[gv]
                        S_.op("act", (lambda ue=ue, pt=pt: lambda e: e.copy(out=ue[:, 2:2 + w], in_=pt[:, 0:w]))(), reads=[pt], writes=[(ue, "m")])
                        S_.op("pool", (lambda ue=ue, cc=cc: lambda e: e.tensor_copy(out=ue[:, 0:2], in_=ucar[:, cc, :]))(), reads=[(ucar, cc)], writes=[(ue, "h")])
                        S_.op("act", (lambda ac=ac, pt=pt, cc=cc: lambda e: e.activation(out=ac[:, 0:w], in_=pt[:, 0:w], func=AF.Identity, bias=cfb[:, cc:cc + 1], scale=cfw[:, 2, cc:cc + 1]))(), reads=[pt, cfw, cfb], writes=[ac])

                def f2():
                    for gv in range(2):
                        ue, ac, cc = ues[gv], acs[gv], ccs[gv]
                        for kk in range(2):
                            S_.op("dve", (lambda ue=ue, ac=ac, cc=cc, kk=kk: lambda e: e.scalar_tensor_tensor(out=ac[:, 0:w], in0=ue[:, kk:kk + w], scalar=cfw[:, kk, cc:cc + 1], in1=ac[:, 0:w], op0=ALU.mult, op1=ALU.add))(), reads=[(ue, "m"), (ue, "h"), cfw, ac], writes=[ac])

                def f3():
                    for gv in range(2):
                        ue, cc = ues[gv], ccs[gv]
                        S_.op("pool", (lambda ue=ue, cc=cc: lambda e: e.tensor_copy(out=ucar[:, cc, :], in_=ue[:, w:w + 2]))(), reads=[(ue, "m"), (ue, "h")], writes=[(ucar, cc)])
                    S_.op("act", lambda e: e.activation(out=sgt[:, 0:w], in_=acs[0][:, 0:w], func=AF.Silu), reads=[acs[0]], writes=[sgt])

                def f4():
                    S_.op("pool", lambda e: e.tensor_tensor(out=asb[:, 0:w], in0=sgt[:, 0:w], in1=acs[1][:, 0:w], op=ALU.mult), reads=[sgt, acs[1]], writes=[asb])

                def f5():
                    S_.op("sp", lambda e: e.dma_start(out=act_c[:, fc, PAD + rel:PAD + rel + w], in_=asb[:, 0:w]), reads=[asb], writes=[("act", tix, fc)], dma=True, stream=asb)
                return [f0, f1, f2, f3, f4, f5]

            tasks = [prologue_task(0, *dtiles[0])] + [[] for _ in range(8)]
            for tix, (rel, w) in enumerate(dtiles):
                tasks.append(h_task(tix, rel, w))
                for fc in range(22):
                    tasks.append(ffn_task(tix, rel, w, fc))
                    if fc == 8 and tix + 1 < len(dtiles):
                        tasks.append(prologue_task(tix + 1, *dtiles[tix + 1]))
            run_pipeline(tasks)
            S_.emit()

        etiles = ([] if last else [(-HALO, HALO)]) + [(k * 512, 512) for k in range(NTD)]
        with ExitStack() as st2:
            def sb2(name, shape, dt):
                return T(st2.enter_context(nc.sbuf_tensor(uniq(name), list(shape), dt)), name)

            def ps2(name, shape=(128, 512), dt=F32):
                return T(st2.enter_context(nc.psum_tensor(uniq(name), list(shape), dt)), name)

            S_ = Sched(ctx)
            wd = sb2("wd", [128, 22, D], BF16)
            wst = [sb2("wst%d" % i, [128, D], F32) for i in range(2)]
            gf = sb2("gf", [128, 8], F32)
            g1n = sb2("g1n", [128, 8], F32)
            hn = [sb2("hn%d" % i, [128, 8, 512], BF16) for i in range(2)]
            jm = sb2("jm", [128, 1], F32)
            ones_bf = sb2("ones_bf3", [128, 128], BF16)
            at = [sb2("at%d" % i, [128, 22, 512], BF16) for i in range(2)]
            xm = [sb2("xm%d" % i, [128, 8, 512], F32) for i in range(2)]
            xn = [sb2("xn%d" % i, [128, 8, 512], F32) for i in range(2)]
            sqb = sb2("sqb", [128, 8, 512], BF16)
            rstd = sb2("rstd", [128, 512], F32)
            pp = [ps2("pp%d" % i) for i in range(6)]
            pss = ps2("pss")
            S_.op("pool", lambda e: e.memset(ones_bf[:], 1.0), writes=[ones_bf])
            S_.op("sp", lambda e: e.dma_start(out=gf[:], in_=final_g), writes=[gf], dma=True)
            if not last:
                S_.op("sp", lambda e: e.dma_start(out=g1n[:], in_=norm1_g[l + 1]), writes=[g1n], dma=True)
            S_.op("sp", lambda e: e.dma_start(out=jm[:], in_=jmask), writes=[jm], dma=True)
            for kc in range(22):
                stg = wst[kc % 2]
                S_.op("act" if kc % 2 else "sp", (lambda kc=kc, stg=stg: lambda e: e.dma_start(out=stg[:], in_=w_ff_down[l, kc * 128:(kc + 1) * 128, :]))(), writes=[stg], dma=True)
                S_.op("act" if kc % 2 else "dve", (lambda kc=kc, stg=stg: lambda e: (e.copy if kc % 2 else e.tensor_copy)(out=wd[:, kc, :], in_=stg[:]))(), reads=[stg], writes=[(wd, kc)])
            act_c = chunked(actT)
            xmid_c = chunked(xmidT)
            xbuf_c = chunked(xbuf)
            out_c = chunked(outT)
            npp = 0
            def e_loads(tix):
                rel_, w_ = etiles[tix]
                att_, xmt_ = at[tix % 2], xm[tix % 2]
                S_.op("sp", lambda e: e.dma_start(out=att_[:, :, 0:w_], in_=act_c[:, :, PAD + rel_:PAD + rel_ + w_]), writes=[att_], dma=True)
                S_.op("sp", lambda e: e.dma_start(out=xmt_[:, :, 0:w_], in_=xmid_c[:, :, PAD + rel_:PAD + rel_ + w_]), writes=[xmt_], dma=True)

            e_loads(0)
            for tix, (rel, w) in enumerate(etiles):
                att = at[tix % 2]
                xmt = xm[tix % 2]
                xnt = xn[tix % 2]
                if tix + 1 < len(etiles):
                    e_loads(tix + 1)
                for co in range(8):
                    pt = pp[npp % 6]; npp += 1
                    for kc in range(22):
                        S_.op("pe", (lambda kc=kc, co=co, pt=pt, att=att, w=w: lambda e: e.matmul(pt[:, 0:w], lhsT=wd[:, kc, co * 128:(co + 1) * 128], rhs=att[:, kc, 0:w], start=(kc == 0), stop=(kc == 21)))(), reads=[(wd, kc), att], writes=[pt])
                    S_.op("dve", (lambda co=co, pt=pt, xmt=xmt, xnt=xnt, w=w: lambda e: e.tensor_tensor(out=xnt[:, co, 0:w], in0=pt[:, 0:w], in1=xmt[:, co, 0:w], op=ALU.add))(), reads=[pt, xmt], writes=[(xnt, co)])
                    if last or rel >= 0:
                        S_.op("pool", (lambda co=co, xnt=xnt, w=w: lambda e: e.tensor_tensor(out=sqb[:, co, 0:w], in0=xnt[:, co, 0:w], in1=xnt[:, co, 0:w], op=ALU.mult))(), reads=[(xnt, co)], writes=[(sqb, co)])
                    elif rel < 0:
                        S_.op("dve", (lambda co=co, xnt=xnt, w=w: lambda e: e.tensor_scalar(out=xnt[:, co, 0:w], in0=xnt[:, co, 0:w], scalar1=jm[:, 0:1], scalar2=None, op0=ALU.mult))(), reads=[(xnt, co), jm], writes=[(xnt, co)])
                xnkeys = [(xnt, c) for c in range(8)]
                if last:
                    for c in range(8):
                        S_.op("pe", (lambda c=c, w=w: lambda e: e.matmul(pss[:, 0:w], lhsT=ones_bf[:], rhs=sqb[:, c, 0:w], start=(c == 0), stop=(c == 7)))(), reads=[ones_bf, (sqb, c)], writes=[pss])
                    S_.op("act", (lambda w=w: lambda e: e.activation(out=rstd[:, 0:w], in_=pss[:, 0:w], func=AF.Sqrt, bias=EPS, scale=1.0 / D))(), reads=[pss], writes=[rstd])
                    S_.op("dve", (lambda w=w: lambda e: e.reciprocal(out=rstd[:, 0:w], in_=rstd[:, 0:w]))(), reads=[rstd], writes=[rstd])
                    for c in range(8):
                        S_.op("dve", (lambda c=c, xnt=xnt, w=w: lambda e: e.scalar_tensor_tensor(out=xnt[:, c, 0:w], in0=xnt[:, c, 0:w], scalar=gf[:, c:c + 1], in1=rstd[:, 0:w], op0=ALU.mult, op1=ALU.mult))(), reads=[(xnt, c), gf, rstd], writes=[(xnt, c)])
                    S_.op("sp", (lambda rel=rel, w=w, xnt=xnt: lambda e: e.dma_start(out=out_c[:, :, rel:rel + w], in_=xnt[:, :, 0:w]))(), reads=xnkeys, writes=[("xout", tix)], dma=True, stream=xnt)
                else:
                    S_.op("sp", (lambda rel=rel, w=w, xnt=xnt: lambda e: e.dma_start(out=xbuf_c[:, :, PAD + rel:PAD + rel + w], in_=xnt[:, :, 0:w]))(), reads=xnkeys, writes=[("xout", tix)], dma=True, stream=xnt)
                    if rel >= 0:
                        hnt = hn[tix % 2]
                        for c in range(8):
                            S_.op("pe", (lambda c=c, w=w: lambda e: e.matmul(pss[:, 0:w], lhsT=ones_bf[:], rhs=sqb[:, c, 0:w], start=(c == 0), stop=(c == 7)))(), reads=[ones_bf, (sqb, c)], writes=[pss])
                        S_.op("act", (lambda w=w: lambda e: e.activation(out=rstd[:, 0:w], in_=pss[:, 0:w], func=AF.Sqrt, bias=EPS, scale=1.0 / D))(), reads=[pss], writes=[rstd])
                        S_.op("dve", (lambda w=w: lambda e: e.reciprocal(out=rstd[:, 0:w], in_=rstd[:, 0:w]))(), reads=[rstd], writes=[rstd])
                        for c in range(8):
                            S_.op("dve", (lambda c=c, xnt=xnt, hnt=hnt, w=w: lambda e: e.scalar_tensor_tensor(out=hnt[:, c, 0:w], in0=xnt[:, c, 0:w], scalar=g1n[:, c:c + 1], in1=rstd[:, 0:w], op0=ALU.mult, op1=ALU.mult))(), reads=[(xnt, c), g1n, rstd], writes=[(hnt, c)])
                        for c in range(8):
                            S_.op("sp", (lambda rel=rel, w=w, hnt=hnt, c=c: lambda e: e.dma_start(out=hmain_t[c // 2].ap()[(c % 2) * 128:(c % 2 + 1) * 128, rel:rel + w], in_=hnt[:, c, 0:w]))(), reads=[(hnt, cc) for cc in range(8)], writes=[("hmain", tix, c)], dma=True, stream=hnt)
            S_.emit()
        if not last:
            pass

    top.close()
    return nc


def make_consts(S, heads):
    bf = ml_dtypes.bfloat16
    k = np.arange(128)
    tri = (k[:, None] >= k[None, :]).astype(np.float32).astype(bf)
    sbmask = (k[:, None] < k[None, :]).astype(np.float32)
    slopes_all = 2.0 ** (-8.0 * (np.arange(4) + 1.0) / 4)
    nh = len(heads)
    dfmask = np.zeros((nh, 128, 128), np.float32)
    kk = k[:, None]; qq = k[None, :]
    allowed = (kk // 64) <= (qq // 64)
    pos = np.arange(S)
    qaug = np.zeros((nh, 4, S), np.float32)
    kaug = np.zeros((nh, 4, S), np.float32)
    for i, h in enumerate(heads):
        sl = slopes_all[h]
        v = np.where(kk <= qq, 1.0, np.exp(-2.0 * sl * (kk - qq)))
        dfmask[i] = np.where(allowed, v, 0.0)
        qaug[i, 0] = 1.0; qaug[i, 1] = 1.0
        qaug[i, 2] = -sl * (pos % 128); qaug[i, 3] = -sl * 128 * (pos // 128)
        kaug[i, 0] = sl * (pos % 128); kaug[i, 1] = sl * 128 * (pos // 128)
        kaug[i, 2] = 1.0; kaug[i, 3] = 1.0
    return {"c_tri": tri, "c_sbmask": sbmask, "c_dfmask": dfmask,
            "c_qaug": qaug.astype(bf), "c_kaug": kaug.astype(bf)}


_WNAMES = ["norm1_g", "w_in", "conv_a_w", "conv_a_b", "w_rgate", "b_rgate", "w_igate", "b_igate",
           "lru_lambda", "lam_q1", "lam_k1", "lam_q2", "lam_k2", "subln_g", "w_out", "norm2_g",
           "w_ff_up", "conv_ff_w", "conv_ff_b", "w_ff_down", "final_g"]


def prep_params(inputs, j):
    p = {n: np.asarray(inputs[n], dtype=np.float32) for n in _WNAMES}
    NL = p["w_in"].shape[0]

    def pc(a):
        sh = a.shape
        return np.ascontiguousarray(np.swapaxes(a.reshape(sh[:-1] + (sh[-1] // 128, 128)), -1, -2))

    a0 = j * 256
    cols = np.concatenate([np.arange(a0, a0 + 256), 512 + np.arange(a0, a0 + 256)] +
                          [base + j * 128 + np.arange(128) for base in (1024, 1280, 1536, 1792, 2048, 2304)])
    p["w_in"] = p["w_in"][:, :, cols]
    for n in ("conv_a_b", "b_rgate", "b_igate", "lru_lambda"):
        p[n] = pc(p[n][:, a0:a0 + 256])
    p["conv_a_w"] = p["conv_a_w"][:, :, a0:a0 + 256]
    p["w_rgate"] = p["w_rgate"][:, 4 * j:4 * j + 4]
    p["w_igate"] = p["w_igate"][:, 4 * j:4 * j + 4]
    for n in ("norm1_g", "norm2_g", "conv_ff_b", "final_g"):
        p[n] = pc(p[n])
    for n in ("conv_a_w", "conv_ff_w"):
        a = p[n]
        k = a.shape[1]
        p[n] = np.ascontiguousarray(a.reshape(NL, k, -1, 128).transpose(0, 3, 1, 2))
    own = [np.concatenate([r * 256 + np.arange(256), 512 + r * 128 + np.arange(128), 768 + r * 128 + np.arange(128)]) for r in range(2)]
    rows = np.concatenate([own[r][i * 64:(i + 1) * 64] for i in range(8) for r in range(2)])
    p["w_out"] = p["w_out"][:, rows, :]
    return {n: np.ascontiguousarray(v) for n, v in p.items()}


def kernel(**inputs):
    x = np.asarray(inputs["x"], dtype=np.float32)
    B, S, _ = x.shape
    NL = int(np.asarray(inputs["w_in"]).shape[0])
    PAD = 64
    HS = S // 2
    nc = build_program(S, NL)
    n_cores = 8
    per_j = []
    for j in range(2):
        m = prep_params(inputs, j)
        m.update(make_consts(S, [2 * j, 2 * j + 1]))
        m["jmask"] = np.full((128, 1), float(j), np.float32)
        per_j.append(m)
    in_maps = []
    for c in range(n_cores):
        b, j = (c // 2) % B, c % 2
        m = dict(per_j[j])
        xt = np.ascontiguousarray(x[b].T)
        m["xT"] = xt
        xhalf = np.zeros((D, PAD + HS), np.float32)
        lo = j * HS
        npre = min(PAD, lo)
        xhalf[:, PAD - npre:] = xt[:, lo - npre:lo + HS]
        m["xh"] = xhalf
        in_maps.append(m)
    res = run_bass_kernel_spmd(nc, in_maps, core_ids=list(range(n_cores)))
    out = np.empty((B, S, D), np.float32)
    for b in range(B):
        for j in range(2):
            out[b, j * HS:(j + 1) * HS, :] = res.results[2 * b + j]["outT"].T
    return out
```
